# Optimizing a Trainium2 kernel written in Bass

```python
import math
import jax, jax.numpy as jnp
from jax import lax
import numpy as np

D_MODEL = 1024
BATCH = 8
SEQ = 4096
DEPTH = 1

D_SSM = D_MODEL // 2
SSM_GROUP = 16
N_SSM_GROUPS = D_SSM // SSM_GROUP
SSM_STATE = 64
SSM_DT_MIN = 1e-3
SSM_DT_MAX = 1e-1
D_ATTN = D_MODEL // 2
DIFF_HEAD_DIM = 64
N_DIFF_HEADS = D_ATTN // (2 * DIFF_HEAD_DIM)
ROPE_THETA = 10000.0
Q_BLOCK = 128
N_BRANCHES = 2
D_IN = D_SSM + 3 * D_ATTN + N_BRANCHES * D_MODEL
D_FF = -(-8 * D_MODEL // (3 * 256)) * 256
DEEPNORM_ALPHA = (2.0 * DEPTH) ** 0.25
DEEPNORM_BETA = (8.0 * DEPTH) ** -0.25
LN_EPS = 1e-5
RMS_EPS = 1e-5

kernel_name = "hybrid_s5_diffattn_gated_deepnorm"


def lambda_init_for(layer_idx):
    return 0.8 - 0.6 * math.exp(-0.3 * layer_idx)


def layer_norm(x, g, b):
    xf = x.astype(jnp.float32)
    mu = jnp.mean(xf, axis=-1, keepdims=True)
    xc = xf - mu
    var = jnp.mean(xc * xc, axis=-1, keepdims=True)
    y = xc * lax.rsqrt(var + LN_EPS) * g.astype(jnp.float32) + b.astype(jnp.float32)
    return y.astype(x.dtype)


def rope_tables(seq_len, dim):
    inv_freq = ROPE_THETA ** (-jnp.arange(0, dim, 2, dtype=jnp.float32) / dim)
    ang = jnp.arange(seq_len, dtype=jnp.float32)[:, None] * inv_freq[None, :]
    return jnp.cos(ang), jnp.sin(ang)


def apply_rope(t, cos, sin):
    half = t.shape[-1] // 2
    t1, t2 = t[..., :half], t[..., half:]
    c = cos[None, :, None, None, :]
    s = sin[None, :, None, None, :]
    return jnp.concatenate([t1 * c - t2 * s, t2 * c + t1 * s], axis=-1)


def s5_branch(u, lam_re, lam_im, log_step, b_re, b_im, c_re, c_im, d_skip, w_glu):
    bsz, seq = u.shape[0], u.shape[1]
    f32 = jnp.float32
    uf = u.astype(f32).reshape(bsz, seq, N_SSM_GROUPS, SSM_GROUP)
    lr, li = lam_re.astype(f32), lam_im.astype(f32)
    dt = jnp.exp(log_step.astype(f32))[:, None]
    mag = jnp.exp(lr * dt)
    abar_re, abar_im = mag * jnp.cos(li * dt), mag * jnp.sin(li * dt)
    nr, ni = abar_re - 1.0, abar_im
    den = lr * lr + li * li
    fr = (nr * lr + ni * li) / den
    fi = (ni * lr - nr * li) / den
    br, bi = b_re.astype(f32), b_im.astype(f32)
    bb_re = fr[..., None] * br - fi[..., None] * bi
    bb_im = fr[..., None] * bi + fi[..., None] * br
    bu_re = jnp.einsum('bsgh,gph->bsgp', uf, bb_re)
    bu_im = jnp.einsum('bsgh,gph->bsgp', uf, bb_im)
    a_re = jnp.broadcast_to(abar_re, bu_re.shape)
    a_im = jnp.broadcast_to(abar_im, bu_im.shape)

    def combine(e1, e2):
        a1r, a1i, b1r, b1i = e1
        a2r, a2i, b2r, b2i = e2
        return (a2r * a1r - a2i * a1i,
                a2r * a1i + a2i * a1r,
                a2r * b1r - a2i * b1i + b2r,
                a2r * b1i + a2i * b1r + b2i)

    _, _, s_re, s_im = lax.associative_scan(combine, (a_re, a_im, bu_re, bu_im), axis=1)
    y = (jnp.einsum('bsgp,ghp->bsgh', s_re, c_re.astype(f32))
         - jnp.einsum('bsgp,ghp->bsgh', s_im, c_im.astype(f32))
         + d_skip.astype(f32) * uf)
    y = jax.nn.gelu(y.reshape(bsz, seq, D_SSM))
    ga = jnp.einsum('bse,ef->bsf', y, w_glu.astype(f32))
    a, g = ga[..., :D_SSM], ga[..., D_SSM:]
    return (a * jax.nn.sigmoid(g)).astype(u.dtype)


def diff_attention(q, k, v, lq1, lk1, lq2, lk2, subln_gain, lambda_init):
    bsz, seq = q.shape[0], q.shape[1]
    f32 = jnp.float32
    qf = q.astype(f32).reshape(bsz, seq, N_DIFF_HEADS, 2, DIFF_HEAD_DIM)
    kf = k.astype(f32).reshape(bsz, seq, N_DIFF_HEADS, 2, DIFF_HEAD_DIM)
    vf = v.astype(f32).reshape(bsz, seq, N_DIFF_HEADS, 2 * DIFF_HEAD_DIM)
    cos, sin = rope_tables(seq, DIFF_HEAD_DIM)
    qf = apply_rope(qf, cos, sin) * (DIFF_HEAD_DIM ** -0.5)
    kf = apply_rope(kf, cos, sin)
    lam = (jnp.exp(jnp.sum(lq1.astype(f32) * lk1.astype(f32)))
           - jnp.exp(jnp.sum(lq2.astype(f32) * lk2.astype(f32))) + lambda_init)
    outs = []
    for i in range(seq // Q_BLOCK):
        q0 = i * Q_BLOCK
        kl = q0 + Q_BLOCK
        qb = qf[:, q0:kl]
        kb = kf[:, :kl]
        vb = vf[:, :kl]
        s = jnp.einsum('bqhcd,bkhcd->bhcqk', qb, kb)
        mask = jnp.arange(kl)[None, :] <= (q0 + jnp.arange(Q_BLOCK))[:, None]
        s = jnp.where(mask, s, -jnp.inf)
        p = jax.nn.softmax(s, axis=-1)
        w = p[:, :, 0] - lam * p[:, :, 1]
        outs.append(jnp.einsum('bhqk,bkhe->bqhe', w, vb))
    o = jnp.concatenate(outs, axis=1)
    ms = jnp.mean(o * o, axis=-1, keepdims=True)
    o = o * lax.rsqrt(ms + RMS_EPS) * subln_gain.astype(f32) * (1.0 - lambda_init)
    return o.reshape(bsz, seq, D_ATTN).astype(q.dtype)


def setup_inputs(seed: int = 0) -> dict:
    key = jax.random.key(seed)
    ks = jax.random.split(key, 32)
    L, D, f32 = DEPTH, D_MODEL, jnp.float32
    G, P, H = N_SSM_GROUPS, SSM_STATE, SSM_GROUP
    nrm = lambda k, shape: jax.random.normal(k, shape, f32)
    x = nrm(ks[0], (BATCH, SEQ, D))
    w_in = nrm(ks[1], (L, D, D_IN)) * D ** -0.5
    b_gate = nrm(ks[2], (L, N_BRANCHES * D)) * 0.01
    ssm_lambda_re = -0.5 + 0.005 * nrm(ks[3], (L, G, P))
    ssm_lambda_im = (math.pi * jnp.arange(P, dtype=f32))[None, None, :] + 0.01 * nrm(ks[4], (L, G, P))
    ssm_log_step = jax.random.uniform(ks[5], (L, G), f32, math.log(SSM_DT_MIN), math.log(SSM_DT_MAX))
    ssm_b_re = nrm(ks[6], (L, G, P, H)) * (2 * H) ** -0.5
    ssm_b_im = nrm(ks[7], (L, G, P, H)) * (2 * H) ** -0.5
    ssm_c_re = nrm(ks[8], (L, G, H, P)) * (2 * P) ** -0.5
    ssm_c_im = nrm(ks[9], (L, G, H, P)) * (2 * P) ** -0.5
    ssm_d = nrm(ks[10], (L, G, H))
    w_glu = nrm(ks[11], (L, D_SSM, 2 * D_SSM)) * D_SSM ** -0.5
    lambda_q1 = nrm(ks[12], (L, DIFF_HEAD_DIM)) * 0.1
    lambda_k1 = nrm(ks[13], (L, DIFF_HEAD_DIM)) * 0.1
    lambda_q2 = nrm(ks[14], (L, DIFF_HEAD_DIM)) * 0.1
    lambda_k2 = nrm(ks[15], (L, DIFF_HEAD_DIM)) * 0.1
    subln_gain = 1.0 + 0.01 * nrm(ks[16], (L, 2 * DIFF_HEAD_DIM))
    w_proj_ssm = nrm(ks[17], (L, D_SSM, D)) * D_SSM ** -0.5
    w_proj_attn = nrm(ks[18], (L, D_ATTN, D)) * D_ATTN ** -0.5
    w_out = nrm(ks[19], (L, D, D)) * D ** -0.5 * DEEPNORM_BETA
    ln1_g = 1.0 + 0.01 * nrm(ks[20], (L, D))
    ln1_b = 0.01 * nrm(ks[21], (L, D))
    w_ffn_gate = nrm(ks[22], (L, D, D_FF)) * D ** -0.5
    w_ffn_up = nrm(ks[23], (L, D, D_FF)) * D ** -0.5
    w_ffn_down = nrm(ks[24], (L, D_FF, D)) * D_FF ** -0.5 * DEEPNORM_BETA
    ln2_g = 1.0 + 0.01 * nrm(ks[25], (L, D))
    ln2_b = 0.01 * nrm(ks[26], (L, D))
    return {"x": x, "w_in": w_in, "b_gate": b_gate,
            "ssm_lambda_re": ssm_lambda_re, "ssm_lambda_im": ssm_lambda_im,
            "ssm_log_step": ssm_log_step, "ssm_b_re": ssm_b_re, "ssm_b_im": ssm_b_im,
            "ssm_c_re": ssm_c_re, "ssm_c_im": ssm_c_im, "ssm_d": ssm_d, "w_glu": w_glu,
            "lambda_q1": lambda_q1, "lambda_k1": lambda_k1, "lambda_q2": lambda_q2,
            "lambda_k2": lambda_k2, "subln_gain": subln_gain,
            "w_proj_ssm": w_proj_ssm, "w_proj_attn": w_proj_attn, "w_out": w_out,
            "ln1_g": ln1_g, "ln1_b": ln1_b,
            "w_ffn_gate": w_ffn_gate, "w_ffn_up": w_ffn_up, "w_ffn_down": w_ffn_down,
            "ln2_g": ln2_g, "ln2_b": ln2_b}


def reference(x, w_in, b_gate, ssm_lambda_re, ssm_lambda_im, ssm_log_step, ssm_b_re, ssm_b_im,
              ssm_c_re, ssm_c_im, ssm_d, w_glu, lambda_q1, lambda_k1, lambda_q2, lambda_k2,
              subln_gain, w_proj_ssm, w_proj_attn, w_out, ln1_g, ln1_b,
              w_ffn_gate, w_ffn_up, w_ffn_down, ln2_g, ln2_b):
    bsz, seq = x.shape[0], x.shape[1]
    o_q = D_SSM
    o_k = o_q + D_ATTN
    o_v = o_k + D_ATTN
    o_g = o_v + D_ATTN
    for l in range(DEPTH):
        lambda_init = lambda_init_for(l)
        h = jnp.einsum('bsd,de->bse', x, w_in[l])
        u = h[..., :o_q]
        q = h[..., o_q:o_k]
        k = h[..., o_k:o_v]
        v = h[..., o_v:o_g]
        gates = jax.nn.sigmoid((h[..., o_g:] + b_gate[l]).astype(jnp.float32))
        gates = gates.reshape(bsz, seq, N_BRANCHES, D_MODEL)
        y_ssm = s5_branch(u, ssm_lambda_re[l], ssm_lambda_im[l], ssm_log_step[l],
                          ssm_b_re[l], ssm_b_im[l], ssm_c_re[l], ssm_c_im[l], ssm_d[l], w_glu[l])
        y_att = diff_attention(q, k, v, lambda_q1[l], lambda_k1[l], lambda_q2[l], lambda_k2[l],
                               subln_gain[l], lambda_init)
        p_ssm = jnp.einsum('bse,ed->bsd', y_ssm, w_proj_ssm[l]).astype(jnp.float32)
        p_att = jnp.einsum('bse,ed->bsd', y_att, w_proj_attn[l]).astype(jnp.float32)
        merged = (gates[:, :, 0] * p_ssm + gates[:, :, 1] * p_att).astype(x.dtype)
        mix = jnp.einsum('bsd,de->bse', merged, w_out[l])
        x = layer_norm(DEEPNORM_ALPHA * x + mix, ln1_g[l], ln1_b[l])
        a = jnp.einsum('bsd,df->bsf', x, w_ffn_gate[l])
        b = jnp.einsum('bsd,df->bsf', x, w_ffn_up[l])
        ff = jnp.einsum('bsf,fd->bsd', jax.nn.silu(a) * b, w_ffn_down[l])
        x = layer_norm(DEEPNORM_ALPHA * x + ff, ln2_g[l], ln2_b[l])
    return x
```

```python
import math
import contextlib
import numpy as np
import concourse.bass as bass
import concourse.mybir as mybir
from concourse.bass_utils import run_bass_kernel_spmd

F32 = mybir.dt.float32
BF16 = mybir.dt.bfloat16
I32 = mybir.dt.int32
AF = mybir.ActivationFunctionType
ALU = mybir.AluOpType

SEQ = 4096
DM = 1024
DFF = 2816
NT = 8
G = 32
P_ = 64
H_ = 16
ALPHA = 2.0 ** 0.25
LAMBDA_INIT = 0.8 - 0.6 * math.exp(0.0)
TWO_PI_LO = 6.283185
INV_2PI = 1.0 / (2.0 * math.pi)


class T:
    __slots__ = ("name", "w", "r")

    def __init__(self, name):
        self.name = name
        self.w = {}
        self.r = {}


class Sched:
    ENGS = ("pe", "act", "dve", "pool", "sp")

    def __init__(self, nc, n_dma_sems=48):
        self.nc = nc
        self.ops = []
        self.n_dma_sems = n_dma_sems
        self.dma_rr = 0
        self.dma_last = {}
        self.last = {}
        self.n_sw = 0
        self.bg = set()

    def _deps(self, key, reads, writes):
        deps = set()
        for t in reads:
            for k, o in t.w.items():
                deps.add(o)
        for t in writes:
            for k, o in t.w.items():
                deps.add(o)
            for k, o in t.r.items():
                deps.add(o)
        return deps

    def op(self, eng, fn, reads=(), writes=()):
        oid = len(self.ops)
        deps = self._deps(eng, reads, writes)
        if eng == "pe":
            deps = {d for d in deps if self.ops[d][0] != "pe" or self.ops[d][3] is not None}
        self.ops.append((eng, fn, deps, None))
        for t in reads:
            t.r[eng] = oid
        for t in writes:
            t.w[eng] = oid
        self.last[eng] = oid
        return oid

    def dma(self, eng, out, in_, reads=(), writes=(), **kw):
        oid = len(self.ops)
        if eng == "pool":
            s = ("sw", self.n_sw)
            self.n_sw += 1
        else:
            s = self.dma_rr
            self.dma_rr = (self.dma_rr + 1) % self.n_dma_sems
        key = ("dma", s)
        deps = self._deps(key, reads, writes)
        if s in self.dma_last:
            deps.add(self.dma_last[s])
        self.dma_last[s] = oid

        kw2 = {k: v for k, v in kw.items() if k != "_bg"}

        def fn(e, out=out, in_=in_, kw=kw2):
            return e.dma_start(out=out, in_=in_, **kw)
        self.ops.append((eng, fn, deps, s))
        if kw.get("_bg"):
            self.bg.add(oid)
        for t in reads:
            t.r[key] = oid
        for t in writes:
            t.w[key] = oid
        return oid

    def barrier(self):
        deps = set(self.last.values()) | set(v for v in self.dma_last.values() if v not in self.bg)
        for eng in self.ENGS:
            self.ops.append((eng, None, set(deps), None))

    def emit(self, final_wait_ops=()):
        nc = self.nc
        ops = self.ops
        n = len(ops)
        needed = [False] * n
        for (eng, fn, deps, ds) in ops:
            for d in deps:
                needed[d] = True
        for d in final_wait_ops:
            needed[d] = True
        semval = [None] * n
        cnt = {}
        for i, (eng, fn, deps, ds) in enumerate(ops):
            if ds is not None:
                key = ("dma", ds)
                cnt[key] = cnt.get(key, 0) + 16
                semval[i] = (key, cnt[key])
            elif needed[i]:
                cnt[eng] = cnt.get(eng, 0) + 1
                semval[i] = (eng, cnt[eng])
        with contextlib.ExitStack() as st:
            sems = {}
            for e in self.ENGS:
                sems[e] = st.enter_context(nc.semaphore("s_" + e))
            for s in range(self.n_dma_sems):
                sems[("dma", s)] = st.enter_context(nc.semaphore("s_dma%d" % s))
            for s in range(self.n_sw):
                sems[("dma", ("sw", s))] = st.enter_context(nc.semaphore("s_sw%d" % s))
            block = st.enter_context(nc.Block())
            seen = {e: {} for e in self.ENGS}
            per_eng = {e: [] for e in self.ENGS}
            for i, (eng, fn, deps, ds) in enumerate(ops):
                waits = {}
                for d in deps:
                    k, v = semval[d]
                    if waits.get(k, 0) < v:
                        waits[k] = v
                wl = []
                for k, v in waits.items():
                    if k == eng and fn is None:
                        continue
                    if seen[eng].get(k, 0) < v:
                        seen[eng][k] = v
                        wl.append((k, v))
                per_eng[eng].append((wl, fn, semval[i]))
            fw = {}
            for d in final_wait_ops:
                k, v = semval[d]
                if fw.get(k, 0) < v:
                    fw[k] = v

            def run(e, lst, extra=None):
                for (wl, fn, sv) in lst:
                    for (k, v) in wl:
                        e.wait_ge(sems[k], v)
                    if fn is None:
                        continue
                    ins = fn(e)
                    if sv is not None:
                        k, v = sv
                        ins.then_inc(sems[k], 16 if isinstance(k, tuple) else 1)
                if extra:
                    for k, v in extra.items():
                        e.wait_ge(sems[k], v)

            @block.tensor
            def _(e):
                run(e, per_eng["pe"])

            @block.scalar
            def _(e):
                run(e, per_eng["act"])

            @block.vector
            def _(e):
                run(e, per_eng["dve"])

            @block.gpsimd
            def _(e):
                run(e, per_eng["pool"])

            @block.sync
            def _(e):
                run(e, per_eng["sp"], fw)


class Arena:
    def __init__(self, ap, words):
        self.ap = ap
        self.words = words
        self.pos = 0
        self.peak = 0

    def alloc(self, n, dt=F32):
        if dt == BF16:
            w = (n + 1) // 2
        else:
            w = n
        assert self.pos + w <= self.words, ("arena overflow", self.pos, w, self.words)
        v = self.ap[:, self.pos:self.pos + w]
        self.pos += w
        self.peak = max(self.peak, self.pos)
        if dt != F32:
            v = v.bitcast(dt)
        return v

    def mark(self):
        return self.pos

    def release(self, m):
        self.pos = m


def v3(ap, a):
    return ap.rearrange("p (a b) -> p a b", a=a)


def build(stop_after=None, dbg=()):
    nc = bass.Bass("TRN2", target_bir_lowering=False)

    def din(name, shape, dt=F32):
        return nc.dram_tensor(name, list(shape), dt, kind="ExternalInput").ap()

    x = din("x", [SEQ, DM])
    w_in = din("w_in", [DM, 4096])
    bgate = din("bgate", [128, 16])
    rl = {k: din("rl_" + k, [128, 256]) for k in ("lre", "lim", "lst", "bre", "bim")}
    d_rl = din("d_rl", [128, 4])
    sl = {k: din("sl_" + k, [128, 16]) for k in ("lre", "lim", "lst")}
    sl_cre = din("sl_cre", [128, 256])
    sl_cim = din("sl_cim", [128, 256])
    gl = {k: din("gl_" + k, [128, 32]) for k in ("lre", "lim", "lst")}
    gl_xb = din("gl_xb", [128, 512])
    gl_yb = din("gl_yb", [128, 512])
    gl_cc = din("gl_cc", [128, 512])
    w_glu = din("w_glu", [512, 1024])
    lamq1 = din("lamq1", [1, 64]); lamk1 = din("lamk1", [1, 64])
    lamq2 = din("lamq2", [1, 64]); lamk2 = din("lamk2", [1, 64])
    subln = din("subln", [128, 1])
    w_ps = din("w_ps", [512, DM])
    w_pa = din("w_pa", [512, DM])
    w_out = din("w_out", [DM, DM])
    ln1g = din("ln1g", [1, DM]); ln1b = din("ln1b", [1, DM])
    ln2g = din("ln2g", [1, DM]); ln2b = din("ln2b", [1, DM])
    w_fg = din("w_fg", [DM, DFF]); w_fu = din("w_fu", [DM, DFF]); w_fd = din("w_fd", [DFF, DM])
    out = nc.dram_tensor("out", [SEQ, DM], F32, kind="ExternalOutput").ap()

    def dscr(name, shape):
        return nc.dram_tensor(name, list(shape), BF16, kind="Internal").ap()
    s_x = dscr("s_x", [SEQ, DM])
    s_qkv = dscr("s_qkv", [DM, 1536])
    s_glu = dscr("s_glu", [512, 1024])
    s_g = dscr("s_g", [DM, 2048])
    s_p = dscr("s_p", [512, 2048])
    s_o = dscr("s_o", [DM, DM])
    s_fg = dscr("s_fg", [DM, DFF]); s_fu = dscr("s_fu", [DM, DFF]); s_fd = dscr("s_fd", [DFF, DM])

    dbg_outs = {}

    S = Sched(nc)
    fin = []
    st = contextlib.ExitStack()
    with st:
        AW = 53100
        arena_t = st.enter_context(nc.sbuf_tensor("arena", [128, AW], F32))
        A = Arena(arena_t, AW)
        banks = [st.enter_context(nc.psum_tensor("bank%d" % i, [128, 512], F32)) for i in range(8)]
        tb = [T("bank%d" % i) for i in range(8)]
        rr = [0]

        def nb():
            i = rr[0]
            rr[0] = (rr[0] + 1) % 8
            return i

        def MM(o, lhsT, rhs, start, stop, reads, writes, **kw):
            S.op("pe", lambda e: e.matmul(o, lhsT=lhsT, rhs=rhs, start=start, stop=stop, **kw), reads, writes)

        def TR(o, in_, ident, reads, writes):
            S.op("pe", lambda e: e.transpose(o, in_, ident), reads, writes)

        def TT(eng, o, a, b, op, reads, writes):
            S.op(eng, lambda e: e.tensor_tensor(out=o, in0=a, in1=b, op=op), reads, writes)

        def TS(eng, o, a, s1, s2, op0, op1, reads, writes):
            if s2 is None:
                S.op(eng, lambda e: e.tensor_scalar(out=o, in0=a, scalar1=s1, scalar2=None, op0=op0), reads, writes)
            else:
                S.op(eng, lambda e: e.tensor_scalar(out=o, in0=a, scalar1=s1, scalar2=s2, op0=op0, op1=op1), reads, writes)

        def STT(eng, o, a, s, b, op0, op1, reads, writes):
            S.op(eng, lambda e: e.scalar_tensor_tensor(out=o, in0=a, scalar=s, in1=b, op0=op0, op1=op1), reads, writes)

        def ACTF(o, in_, func, reads, writes, bias=None, scale=1.0):
            if bias is None:
                S.op("act", lambda e: e.activation(out=o, in_=in_, func=func, scale=scale), reads, writes)
            else:
                S.op("act", lambda e: e.activation(out=o, in_=in_, func=func, bias=bias, scale=scale), reads, writes)

        def CP(eng, o, in_, reads, writes):
            if eng == "act":
                S.op("act", lambda e: e.copy(out=o, in_=in_), reads, writes)
            else:
                S.op(eng, lambda e: e.tensor_copy(out=o, in_=in_), reads, writes)

        def MS(eng, o, val, writes):
            S.op(eng, lambda e: e.memset(o, val), (), writes)

        def dump(name, ap, shape, dt, t):
            if name in dbg:
                S.barrier()
                d = nc.dram_tensor("dbg_" + name, list(shape), dt, kind="ExternalOutput").ap()
                dbg_outs[name] = (shape, dt)
                fin.append(S.dma("sp", d, ap, reads=[t]))

        tC = T("consts")
        ident = A.alloc(128, BF16)
        MS("pool", ident, 0.0, [tC])
        S.op("pool", lambda e: e.affine_select(out=ident, in_=ident, pattern=[[-1, 128]], compare_op=ALU.not_equal,
                                               fill=1.0, base=0, channel_multiplier=1), [tC], [tC])
        ones_bf = A.alloc(128, BF16)
        MS("pool", ones_bf, 1.0, [tC])
        onesm_bf = A.alloc(128, BF16)
        MS("pool", onesm_bf, 1.0 / 128.0, [tC])
        tri = A.alloc(128, BF16)
        MS("pool", tri, 1.0, [tC])
        S.op("pool", lambda e: e.affine_select(out=tri, in_=tri, pattern=[[1, 128]], compare_op=ALU.is_ge,
                                               fill=0.0, base=0, channel_multiplier=-1), [tC], [tC])
        zero_c = A.alloc(1); MS("dve", zero_c, 0.0, [tC])
        eps_c = A.alloc(1); MS("dve", eps_c, 1e-5, [tC])
        pidx_i = A.alloc(1).bitcast(I32)
        S.op("pool", lambda e: e.iota(pidx_i, pattern=[[0, 1]], base=0, channel_multiplier=1), (), [tC])
        tmp_i = A.alloc(1).bitcast(I32)
        odd_c = A.alloc(1)
        even_c = A.alloc(1)
        sgn_c = A.alloc(1)
        nsgn_c = A.alloc(1)
        rowg_c = A.alloc(1)
        S.op("dve", lambda e: e.tensor_single_scalar(out=tmp_i, in_=pidx_i, scalar=4, op=ALU.arith_shift_right), [tC], [tC])
        CP("dve", rowg_c, tmp_i, [tC], [tC])
        S.op("dve", lambda e: e.tensor_single_scalar(out=tmp_i, in_=tmp_i, scalar=1, op=ALU.bitwise_and), [tC], [tC])
        CP("dve", odd_c, tmp_i, [tC], [tC])
        TS("dve", even_c, odd_c, -1.0, 1.0, ALU.mult, ALU.add, [tC], [tC])
        S.op("dve", lambda e: e.tensor_single_scalar(out=tmp_i, in_=pidx_i, scalar=6, op=ALU.arith_shift_right), [tC], [tC])
        CP("dve", sgn_c, tmp_i, [tC], [tC])
        TS("dve", sgn_c, sgn_c, 2.0, -1.0, ALU.mult, ALU.add, [tC], [tC])
        TS("dve", nsgn_c, sgn_c, -1.0, None, ALU.mult, None, [tC], [tC])

        tCd = T("consts_dma")
        bg_sb = A.alloc(16)
        S.dma("sp", bg_sb, bgate, writes=[tCd])
        subln_c = A.alloc(1)
        S.dma("sp", subln_c, subln, writes=[tCd])
        lq = A.alloc(256)
        for i, src in enumerate((lamq1, lamk1, lamq2, lamk2)):
            S.dma("sp", lq[:, i * 64:(i + 1) * 64], src.to_broadcast([128, 64]), writes=[tCd])
        lsum = A.alloc(2)
        lprod = A.alloc(128)
        nlam_c = A.alloc(1)

        def late_consts():
            TS("dve", subln_c, subln_c, 1.0 - LAMBDA_INIT, None, ALU.mult, None, [tCd], [tCd])
            TT("dve", lprod[:, 0:64], lq[:, 0:64], lq[:, 64:128], ALU.mult, [tCd], [tCd])
            TT("dve", lprod[:, 64:128], lq[:, 128:192], lq[:, 192:256], ALU.mult, [tCd], [tCd])
            S.op("dve", lambda e: e.reduce_sum(out=lsum, in_=v3(lprod, 2), axis=mybir.AxisListType.X), [tCd], [tCd])
            ACTF(lsum, lsum, AF.Exp, [tCd], [tCd])
            TT("dve", nlam_c, lsum[:, 1:2], lsum[:, 0:1], ALU.subtract, [tCd], [tCd])
            TS("dve", nlam_c, nlam_c, -LAMBDA_INIT, None, ALU.add, None, [tCd], [tC])

        tSg, tSp, tSo, tSfg, tSfu, tSfd = (T("s_g"), T("s_p"), T("s_o"), T("s_fg"), T("s_fu"), T("s_fd"))

        def cast_weights():
            sg4 = s_g.rearrange("k (d b c) -> k d b c", d=8, b=2)
            for br in range(2):
                for r in range(2):
                    rs = slice(r * 512, (r + 1) * 512)
                    S.dma("pool", sg4[rs, :, br, :],
                          w_in[rs, 2048 + br * 1024: 2048 + (br + 1) * 1024].rearrange("k (d c) -> k d c", d=8), writes=[tSg], _bg=True)
            sp4 = s_p.rearrange("k (d b c) -> k d b c", d=8, b=2)
            S.dma("pool", sp4[:, :, 0, :], w_ps.rearrange("k (d c) -> k d c", d=8), writes=[tSp], _bg=True)
            S.dma("pool", sp4[:, :, 1, :], w_pa.rearrange("k (d c) -> k d c", d=8), writes=[tSp], _bg=True)
            for r in range(2):
                S.dma("pool", s_o[r * 512:(r + 1) * 512, :], w_out[r * 512:(r + 1) * 512, :], writes=[tSo], _bg=True)
            for r in range(2):
                S.dma("pool", s_fg[r * 512:(r + 1) * 512, :], w_fg[r * 512:(r + 1) * 512, :], writes=[tSfg], _bg=True)
                S.dma("pool", s_fu[r * 512:(r + 1) * 512, :], w_fu[r * 512:(r + 1) * 512, :], writes=[tSfu], _bg=True)
            for r in range(4):
                S.dma("pool", s_fd[r * 704:(r + 1) * 704, :], w_fd[r * 704:(r + 1) * 704, :], writes=[tSfd], _bg=True)

        AT = A.alloc(4 * 4096, BF16); tAT = T("AT")
        m_keep = A.mark()
        uT = A.alloc(4 * 4096, BF16); tuT = T("uT")
        m_persist = A.mark()

        wu = A.alloc(8 * 512, BF16); twu = T("wu")
        tSx = [T("s_x%d" % j) for j in range(NT)]
        tSqkv = T("s_qkv"); tSglu = T("s_glu")
        xst = [A.alloc(4096) for _ in range(2)]; txst = [T("xst0"), T("xst1")]
        S.dma("sp", v3(xst[1], 8), w_in[:, 0:512].rearrange("(kt p) c -> p kt c", p=128), writes=[txst[1]])
        CP("dve", wu[:, 0:2048], xst[1][:, 0:2048], [txst[1]], [twu])
        CP("act", wu[:, 2048:4096], xst[1][:, 2048:4096], [txst[1]], [twu])

        def load_xT(j, xbf, txbf, xT, txT, first_pass=False):
            if first_pass:
                st_ = xst[j % 2]; tst_ = txst[j % 2]
                S.dma("sp", v3(st_, 4), x[j * 512:(j + 1) * 512, :].rearrange("(b p) d -> p b d", p=128), writes=[tst_])
                CP("act", xbf[:, 0:1024], st_[:, 0:1024], [tst_], [txbf])
                CP("dve", xbf[:, 1024:2048], st_[:, 1024:2048], [tst_], [txbf])
                CP("act", xbf[:, 2048:3072], st_[:, 2048:3072], [tst_], [txbf])
                CP("dve", xbf[:, 3072:4096], st_[:, 3072:4096], [tst_], [txbf])
                S.dma("sp", s_x[j * 512:(j + 1) * 512, :].rearrange("(b p) d -> p b d", p=128), v3(xbf, 4),
                      reads=[txbf], writes=[tSx[j]])
            else:
                S.dma("sp", v3(xbf, 4), s_x[j * 512:(j + 1) * 512, :].rearrange("(b p) d -> p b d", p=128),
                      reads=[tSx[j]], writes=[txbf])
            for dt_ in range(8):
                bi = nb()
                pb = banks[bi][:].bitcast(BF16)
                for b in range(4):
                    TR(pb[:, b * 128:(b + 1) * 128], xbf[:, b * 1024 + dt_ * 128: b * 1024 + (dt_ + 1) * 128], ident,
                       [txbf, tC], [tb[bi]])
                CP("act" if dt_ % 2 == 0 else "dve", xT[:, dt_ * 512:(dt_ + 1) * 512], pb[:, 0:512], [tb[bi]], [txT])

        xbfs = [A.alloc(4096, BF16) for _ in range(2)]; txbfs = [T("xbf0"), T("xbf1")]
        xTs = [A.alloc(4096, BF16) for _ in range(2)]; txTs = [T("xT0"), T("xT1")]
        load_xT(0, xbfs[0], txbfs[0], xTs[0], txTs[0], first_pass=True)
        for j in range(NT):
            xbf, txbf, xT, txT = xbfs[j % 2], txbfs[j % 2], xTs[j % 2], txTs[j % 2]
            if j + 1 < NT:
                load_xT(j + 1, xbfs[(j + 1) % 2], txbfs[(j + 1) % 2], xTs[(j + 1) % 2], txTs[(j + 1) % 2], first_pass=True)
            for gt in range(4):
                bi = nb()
                for kt in range(8):
                    MM(banks[bi][:], wu[:, kt * 512 + gt * 128: kt * 512 + (gt + 1) * 128], xT[:, kt * 512:(kt + 1) * 512],
                       kt == 0, kt == 7, [twu, txT], [tb[bi]])
                dst = v3(uT[:, gt * 4096:(gt + 1) * 4096], 8)[:, :, j * 64:(j + 1) * 64]
                src = banks[bi][:].rearrange("p (c i) -> p i c", i=8)
                CP("dve" if gt % 2 == 0 else "act", dst, src, [tb[bi]], [tuT])
        dump("uT", uT, [128, 4 * 4096], BF16, tuT)
        S.barrier()
        A.release(m_persist)
        late_consts()
        if stop_after == "A_u":
            S.emit(fin)
            return nc, dbg_outs

        MUL, ADD, SUB = ALU.mult, ALU.add, ALU.subtract

        def sincos(theta, n, tP, scale_in=INV_2PI):
            y = A.alloc(n); yi = A.alloc(n).bitcast(I32); yf = A.alloc(n); r = A.alloc(n)
            sn = A.alloc(n); cs = A.alloc(n)
            TS("dve", y, theta, scale_in, None, MUL, None, [tP], [tP])
            CP("dve", yi, y, [tP], [tP]); CP("dve", yf, yi, [tP], [tP])
            TT("dve", r, y, yf, SUB, [tP], [tP])
            ACTF(sn, r, AF.Sin, [tP], [tP], scale=TWO_PI_LO)
            TS("dve", y, y, 0.25, None, ADD, None, [tP], [tP])
            CP("dve", yi, y, [tP], [tP]); CP("dve", yf, yi, [tP], [tP])
            TT("dve", r, y, yf, SUB, [tP], [tP])
            ACTF(cs, r, AF.Sin, [tP], [tP], scale=TWO_PI_LO)
            return sn, cs

        def ssm_params(srcs, npow):
            tP = T("ssmp")
            n = sum(k for _, k in srcs)
            lr = A.alloc(n); li = A.alloc(n); ls = A.alloc(n)
            off = 0
            for src, k in srcs:
                S.dma("sp", lr[:, off:off + k], src["lre"], writes=[tP]); S.dma("sp", li[:, off:off + k], src["lim"], writes=[tP])
                S.dma("sp", ls[:, off:off + k], src["lst"], writes=[tP])
                off += k
            dt_ = A.alloc(n); ACTF(dt_, ls, AF.Exp, [tP], [tP])
            ldr = A.alloc(n); TT("dve", ldr, lr, dt_, MUL, [tP], [tP])
            ldi = A.alloc(n); TT("dve", ldi, li, dt_, MUL, [tP], [tP])
            mag = A.alloc(n); ACTF(mag, ldr, AF.Exp, [tP], [tP])
            sn, cs = sincos(ldi, n, tP)
            ar = A.alloc(n); ai = A.alloc(n)
            TT("dve", ar, mag, cs, MUL, [tP], [tP]); TT("dve", ai, mag, sn, MUL, [tP], [tP])
            den = A.alloc(n); t0 = A.alloc(n); t1 = A.alloc(n)
            TT("dve", den, lr, lr, MUL, [tP], [tP]); TT("dve", t0, li, li, MUL, [tP], [tP])
            TT("dve", den, den, t0, ADD, [tP], [tP])
            S.op("dve", lambda e: e.reciprocal(out=den, in_=den), [tP], [tP])
            nr = A.alloc(n); TS("dve", nr, ar, -1.0, None, ADD, None, [tP], [tP])
            fr = A.alloc(n); fi = A.alloc(n)
            TT("dve", t0, nr, lr, MUL, [tP], [tP]); TT("dve", t1, ai, li, MUL, [tP], [tP])
            TT("dve", t0, t0, t1, ADD, [tP], [tP]); TT("dve", fr, t0, den, MUL, [tP], [tP])
            TT("dve", t0, ai, lr, MUL, [tP], [tP]); TT("dve", t1, nr, li, MUL, [tP], [tP])
            TT("dve", t0, t0, t1, SUB, [tP], [tP]); TT("dve", fi, t0, den, MUL, [tP], [tP])
            pw = []
            one = A.alloc(n); zer = A.alloc(n)
            MS("dve", one, 1.0, [tP]); MS("dve", zer, 0.0, [tP])
            pw.append((one, zer)); pw.append((ar, ai))
            for k in range(2, npow + 1):
                pr_, pi_ = pw[k - 1]
                nr_ = A.alloc(n); ni_ = A.alloc(n)
                TT("dve", t0, pr_, ar, MUL, [tP], [tP]); TT("dve", t1, pi_, ai, MUL, [tP], [tP])
                TT("dve", nr_, t0, t1, SUB, [tP], [tP])
                TT("dve", t0, pr_, ai, MUL, [tP], [tP]); TT("dve", t1, pi_, ar, MUL, [tP], [tP])
                TT("dve", ni_, t0, t1, ADD, [tP], [tP])
                pw.append((nr_, ni_))
            outs = []
            off = 0
            for _, k in srcs:
                sl_ = slice(off, off + k)
                outs.append(dict(tP=tP, fr=fr[:, sl_], fi=fi[:, sl_], mag=mag[:, sl_], ldi=ldi[:, sl_],
                                 pw=[(a_[:, sl_], b_[:, sl_]) for a_, b_ in pw]))
                off += k
            return outs

        Tz = A.alloc(4 * 8 * 128, BF16); tTz = T("Tz")
        Tz4 = Tz.rearrange("p (g t s) -> p g t s", g=4, t=8)
        Ww = A.alloc(16 * 8 * 2 * 32, BF16); tWw = T("Ww")
        Ww5 = Ww.rearrange("p (a j t s) -> p a j t s", a=16, j=8, t=2)
        MS("pool", Ww, 0.0, [tWw])
        r8 = A.alloc(16); psi = A.alloc(16); tsl = T("sl_small")
        cidx = A.alloc(512)
        cidx_i = A.alloc(512).bitcast(I32)
        S.op("pool", lambda e: e.iota(cidx_i, pattern=[[1, 512]], base=0, channel_multiplier=0), (), [tsl])
        CP("dve", cidx, cidx_i, [tsl], [tsl])
        Wz = A.alloc(4 * 8 * 2 * 128, BF16); tWz = T("Wz")
        Wz5 = Wz.rearrange("p (g i t s) -> p g i t s", g=4, i=8, t=2)
        m_tmp = A.mark()

        pr, pg, ps_ = ssm_params([(rl, 256), (gl, 32), (sl, 16)], 8)
        tRL = T("rl_tmp"); tGL = T("gl_tmp"); tSLt = T("sl_tmp")
        tP = pr["tP"]
        bre_t = A.alloc(256); bim_t = A.alloc(256)
        S.dma("sp", bre_t, rl["bre"], writes=[tRL]); S.dma("sp", bim_t, rl["bim"], writes=[tRL])
        bbr = A.alloc(256); bbi = A.alloc(256); t0 = A.alloc(256); t1 = A.alloc(256)
        TT("dve", t0, pr["fr"], bre_t, MUL, [tP, tRL], [tRL]); TT("dve", t1, pr["fi"], bim_t, MUL, [tP, tRL], [tRL])
        TT("dve", bbr, t0, t1, SUB, [tP, tRL], [tRL])
        TT("dve", t0, pr["fr"], bim_t, MUL, [tP, tRL], [tRL]); TT("dve", t1, pr["fi"], bre_t, MUL, [tP, tRL], [tRL])
        TT("dve", bbi, t0, t1, ADD, [tP, tRL], [tRL])
        vr = A.alloc(256); vi = A.alloc(256)
        for i in range(8):
            k = 7 - i
            if k == 0:
                svr, svi = bbr, bbi
            else:
                pwr, pwi = pr["pw"][k]
                TT("dve", t0, pwr, bbr, MUL, [tP, tRL], [tRL]); TT("dve", t1, pwi, bbi, MUL, [tP, tRL], [tRL])
                TT("dve", vr, t0, t1, SUB, [tP, tRL], [tRL])
                TT("dve", t0, pwr, bbi, MUL, [tP, tRL], [tRL]); TT("dve", t1, pwi, bbr, MUL, [tP, tRL], [tRL])
                TT("dve", vi, t0, t1, ADD, [tP, tRL], [tRL])
                svr, svi = vr, vi
            for part, sv in ((0, svr), (1, svi)):
                TS("dve", Wz5[:, :, i, part, 0:64], v3(sv, 4), even_c, None, MUL, None, [tP, tRL, tC], [tWz])
                TS("dve", Wz5[:, :, i, part, 64:128], v3(sv, 4), odd_c, None, MUL, None, [tP, tRL, tC], [tWz])
        dump("Wz", Wz, [128, 8192], BF16, tWz)

        xb = A.alloc(512); yb = A.alloc(512); cc = A.alloc(512); dsb = A.alloc(4)
        S.dma("sp", xb, gl_xb, writes=[tGL]); S.dma("sp", yb, gl_yb, writes=[tGL]); S.dma("sp", cc, gl_cc, writes=[tGL])
        S.dma("sp", dsb, d_rl, writes=[tGL])

        def bc_h(a):
            return a.unsqueeze(2).to_broadcast([128, 32, 16])
        xbb = A.alloc(512); ybb = A.alloc(512); g0 = A.alloc(512); g1 = A.alloc(512)
        TT("dve", g0, v3(xb, 32), bc_h(pg["fr"]), MUL, [tP, tGL], [tGL]); TT("dve", g1, v3(yb, 32), bc_h(pg["fi"]), MUL, [tP, tGL], [tGL])
        STT("dve", xbb, g1, sgn_c, g0, MUL, ADD, [tP, tGL, tC], [tGL])
        TT("dve", g0, v3(yb, 32), bc_h(pg["fr"]), MUL, [tP, tGL], [tGL]); TT("dve", g1, v3(xb, 32), bc_h(pg["fi"]), MUL, [tP, tGL], [tGL])
        STT("dve", ybb, g1, nsgn_c, g0, MUL, ADD, [tP, tGL, tC], [tGL])
        rhsC = A.alloc(512)
        TS("dve", rhsC, cc, nsgn_c, None, MUL, None, [tP, tGL, tC], [tGL])
        colg_i = A.alloc(128).bitcast(I32); colg = A.alloc(128); bdmask = A.alloc(128); identf = A.alloc(128)
        S.op("pool", lambda e: e.iota(colg_i, pattern=[[1, 8], [0, 16]], base=0, channel_multiplier=0), (), [tGL])
        CP("dve", colg, colg_i, [tP, tGL], [tGL])
        TS("dve", bdmask, colg, rowg_c, None, ALU.is_equal, None, [tP, tGL, tC], [tGL])
        CP("dve", identf, ident, [tC], [tGL])
        dm = A.alloc(512)
        for gt in range(4):
            TS("dve", dm[:, gt * 128:(gt + 1) * 128], identf, dsb[:, gt:gt + 1], None, MUL, None, [tP, tGL], [tGL])
        lhs = [A.alloc(512) for _ in range(2)]
        tz0 = A.alloc(512)
        for tau in range(8):
            lt = lhs[tau % 2]
            pwr, pwi = pg["pw"][tau]
            TT("dve", g0, v3(xbb, 32), bc_h(pwr), MUL, [tP, tGL], [tGL]); TT("dve", g1, v3(ybb, 32), bc_h(pwi), MUL, [tP, tGL], [tGL])
            tl = T("lhs%d" % tau)
            STT("dve", lt, g1, sgn_c, g0, MUL, ADD, [tP, tGL, tC], [tl])
            bi = nb()
            for gt in range(4):
                MM(banks[bi][:, gt * 128:(gt + 1) * 128], lt[:, gt * 128:(gt + 1) * 128], rhsC[:, gt * 128:(gt + 1) * 128],
                   True, True, [tl, tP, tGL], [tb[bi]])
            mb = bdmask.unsqueeze(1).to_broadcast([128, 4, 128])
            if tau == 0:
                TT("dve", v3(tz0, 4), v3(banks[bi][:], 4), mb, MUL, [tb[bi], tP, tGL], [tGL])
                TT("dve", Tz4[:, :, 0, :], v3(tz0, 4), v3(dm, 4), ADD, [tP, tGL], [tTz])
            else:
                TT("dve", Tz4[:, :, tau, :], v3(banks[bi][:], 4), mb, MUL, [tb[bi], tP, tGL], [tTz])
            S.op("dve", lambda e: e.memset(g0[:, 0:1], 0.0), [tl], [tGL])
        dump("Tz", Tz, [128, 4096], BF16, tTz)

        cre_t = A.alloc(256); cim_t = A.alloc(256)
        S.dma("sp", cre_t, sl_cre, writes=[tSLt]); S.dma("sp", cim_t, sl_cim, writes=[tSLt])
        m2 = A.alloc(16)
        TT("dve", m2, ps_["mag"], ps_["mag"], MUL, [tP, tSLt], [tSLt]); TT("dve", m2, m2, m2, MUL, [tP, tSLt], [tSLt])
        TT("dve", r8, m2, m2, MUL, [tP, tSLt], [tsl])
        y8 = A.alloc(16); y8i = A.alloc(16).bitcast(I32); y8f = A.alloc(16)
        TS("dve", y8, ps_["ldi"], 8.0 * INV_2PI, None, MUL, None, [tP, tSLt], [tSLt])
        CP("dve", y8i, y8, [tP, tSLt], [tSLt]); CP("dve", y8f, y8i, [tP, tSLt], [tSLt])
        TT("dve", psi, y8, y8f, SUB, [tP, tSLt], [tsl])
        tA_ = A.alloc(256); tB_ = A.alloc(256); tC_ = A.alloc(256); tD_ = A.alloc(256)

        def bc_o(a):
            return a.unsqueeze(2).to_broadcast([128, 16, 16])
        for j in range(8):
            pwr, pwi = ps_["pw"][j + 1]
            TT("pool", tA_, v3(cre_t, 16), bc_o(pwr), MUL, [tP, tSLt], [tSLt]); TT("pool", tB_, v3(cim_t, 16), bc_o(pwi), MUL, [tP, tSLt], [tSLt])
            TT("pool", tC_, v3(cre_t, 16), bc_o(pwi), MUL, [tP, tSLt], [tSLt]); TT("pool", tD_, v3(cim_t, 16), bc_o(pwr), MUL, [tP, tSLt], [tSLt])
            for half in range(2):
                hp = slice(half * 64, (half + 1) * 64)
                cs_ = slice(half * 16, (half + 1) * 16)
                TT("pool", Ww5[hp, :, j, 0, cs_], v3(tA_, 16)[hp], v3(tB_, 16)[hp], SUB, [tP, tSLt], [tWw])
                TT("pool", v3(tC_, 16)[hp], v3(tC_, 16)[hp], v3(tD_, 16)[hp], ADD, [tP, tSLt], [tSLt])
                TS("pool", Ww5[hp, :, j, 1, cs_], v3(tC_, 16)[hp], -1.0, None, MUL, None, [tP, tSLt], [tWw])
        dump("Ww", Ww, [128, 8192], BF16, tWw)
        S.dma("pool", s_glu, w_glu, writes=[tSglu], _bg=True)
        for r in range(2):
            S.dma("pool", s_qkv[r * 512:(r + 1) * 512, :], w_in[r * 512:(r + 1) * 512, 512:2048], writes=[tSqkv], _bg=True)
        S.barrier(); A.release(m_tmp)

        sprev = A.alloc(16 * 2 * 514, BF16); tsp = T("sprev")
        sprev4 = sprev.rearrange("p (a t c) -> p a t c", a=16, t=2)
        MS("dve", sprev, 0.0, [tsp])
        m_ssm = A.mark()
        uT4 = uT.rearrange("p (g i c) -> p g i c", g=4, i=8)
        Ers = [A.alloc(1024) for _ in range(2)]; Eis = [A.alloc(1024) for _ in range(2)]; tEs = [T("E0"), T("E1")]
        Y = A.alloc(1024); Yi = A.alloc(1024).bitcast(I32); Yf = A.alloc(1024); tY = T("Ytmp")
        wk = [[A.alloc(512) for _ in range(8)] for _ in range(2)]
        twk = [T("wk0"), T("wk1")]
        for pb in range(8):
            Er = Ers[pb % 2]; Ei = Eis[pb % 2]; tE = tEs[pb % 2]
            TT("pool", v3(Y, 2), psi[:, pb * 2:(pb + 1) * 2].unsqueeze(2).to_broadcast([128, 2, 512]),
               cidx.unsqueeze(1).to_broadcast([128, 2, 512]), MUL, [tsl], [tY])
            CP("dve", Yi, Y, [tY], [tY]); CP("dve", Yf, Yi, [tY], [tY])
            TT("pool", Yf, Y, Yf, SUB, [tY], [tY])
            ACTF(Ei, Yf, AF.Sin, [tY], [tE], scale=TWO_PI_LO)
            TS("dve", Y, Y, 0.25, None, ADD, None, [tY], [tY])
            CP("dve", Yi, Y, [tY], [tY]); CP("dve", Yf, Yi, [tY], [tY])
            TT("pool", Yf, Y, Yf, SUB, [tY], [tY])
            ACTF(Er, Yf, AF.Sin, [tY], [tE], scale=TWO_PI_LO)
            for q in range(2):
                pt = pb * 2 + q
                rg = (pt % 4) * 32
                gt = pt // 4
                bzr, bzi = nb(), nb()
                for part, bz in ((0, bzr), (1, bzi)):
                    for i in range(8):
                        MM(banks[bz][:], Wz5[rg:rg + 32, gt, i, part, :], uT4[rg:rg + 32, gt, i, :], i == 0, i == 7,
                           [tWz, tuT], [tb[bz]], tile_position=(rg, 0))
                w = wk[pt % 2]; tw = twk[pt % 2]
                er = Er[:, q * 512:(q + 1) * 512]; ei = Ei[:, q * 512:(q + 1) * 512]
                TT("dve", w[0], banks[bzr][:], er, MUL, [tb[bzr], tE], [tw]); TT("dve", w[1], banks[bzi][:], ei, MUL, [tb[bzi], tE], [tw])
                TT("dve", w[2], banks[bzi][:], er, MUL, [tb[bzi], tE], [tw]); TT("dve", w[3], banks[bzr][:], ei, MUL, [tb[bzr], tE], [tw])
                TT("dve", w[4], w[0], w[1], ADD, [tw], [tw])
                TT("pool", w[5], w[2], w[3], SUB, [tw], [tw])
                r8b = r8[:, pt:pt + 1].to_broadcast([128, 512])
                S.op("dve", lambda e, o=w[6], d0=r8b, d1=w[4]: e.tensor_tensor_scan(out=o, data0=d0, data1=d1, initial=0.0, op0=MUL, op1=ADD),
                     [tw, tsl], [tw])
                S.op("dve", lambda e, o=w[7], d0=r8b, d1=w[5]: e.tensor_tensor_scan(out=o, data0=d0, data1=d1, initial=0.0, op0=MUL, op1=ADD),
                     [tw, tsl], [tw])
                TT("dve", w[0], w[6], er, MUL, [tw, tE], [tw]); TT("pool", w[1], w[7], ei, MUL, [tw, tE], [tw])
                TT("dve", w[2], w[7], er, MUL, [tw, tE], [tw]); TT("pool", w[3], w[6], ei, MUL, [tw, tE], [tw])
                TT("dve", sprev4[:, pt, 0, 1:513], w[0], w[1], SUB, [tw], [tsp])
                TT("pool", sprev4[:, pt, 1, 1:513], w[2], w[3], ADD, [tw], [tsp])
        dump("sprev", sprev, [128, 16 * 2 * 514], BF16, tsp)
        S.barrier(); A.release(m_ssm)
        if stop_after == "S3":
            S.emit(fin)
            return nc, dbg_outs

        wglu = A.alloc(4 * 1024, BF16); twg = T("wglu")
        S.dma("sp", v3(wglu, 4), s_glu.rearrange("(kt p) c -> p kt c", p=128), reads=[tSglu], writes=[twg])
        yTg = [A.alloc(4 * 512, BF16) for _ in range(2)]; tyT = [T("yTg0"), T("yTg1")]
        NW = 3
        wk = [[A.alloc(512) for _ in range(5)] for _ in range(NW)]; twk = [T("w4_%d" % i) for i in range(NW)]
        sgb = [A.alloc(512) for _ in range(2)]; tsg = [T("sg0"), T("sg1")]
        AT4 = AT.rearrange("p (o c i) -> p o i c", o=4, i=8)
        cnt4 = [0]

        def s4(j):
            yt = yTg[j % 2]; tyt = tyT[j % 2]
            for gt in range(4):
                bI, bT = nb(), nb()
                for ptl in range(4):
                    pt = gt * 4 + ptl
                    for part in range(2):
                        MM(banks[bI][ptl * 32:(ptl + 1) * 32, :], Ww5[:, pt, j, part, :], sprev4[:, pt, part, 0:512],
                           part == 0, part == 1, [tWw, tsp], [tb[bI]], tile_position=(0, ptl * 32))
                for i in range(j + 1):
                    MM(banks[bT][:], Tz4[:, gt, j - i, :], uT4[:, gt, i, :], i == 0, i == j, [tTz, tuT], [tb[bT]])
                w = wk[cnt4[0] % NW]; tw = twk[cnt4[0] % NW]; cnt4[0] += 1
                CP("act", w[0], banks[bI][:], [tb[bI]], [tw])
                TT("dve", w[1], banks[bT][:], w[0], ADD, [tb[bT], tw], [tw])
                TT("pool", w[2], w[1], w[1], MUL, [tw], [tw])
                TS("pool", w[2], w[2], 0.044715, 1.0, MUL, ADD, [tw], [tw])
                TT("pool", w[3], w[2], w[1], MUL, [tw], [tw])
                ACTF(w[4], w[3], AF.Sigmoid, [tw], [tw], scale=2.0 * math.sqrt(2.0 / math.pi))
                TT("dve", yt[:, gt * 512:(gt + 1) * 512], w[1], w[4], MUL, [tw], [tyt])

        def s5(j):
            yt = yTg[j % 2]; tyt = tyT[j % 2]
            for ot in range(4):
                bA, bG = nb(), nb()
                for kt in range(4):
                    MM(banks[bA][:], wglu[:, kt * 1024 + ot * 128: kt * 1024 + (ot + 1) * 128], yt[:, kt * 512:(kt + 1) * 512],
                       kt == 0, kt == 3, [twg, tyt], [tb[bA]])
                for kt in range(4):
                    MM(banks[bG][:], wglu[:, kt * 1024 + 512 + ot * 128: kt * 1024 + 512 + (ot + 1) * 128],
                       yt[:, kt * 512:(kt + 1) * 512], kt == 0, kt == 3, [twg, tyt], [tb[bG]])
                sg = sgb[ot % 2]; ts_ = tsg[ot % 2]
                ACTF(sg, banks[bG][:], AF.Sigmoid, [tb[bG]], [ts_])
                TT("dve", AT4[:, ot, j, :], banks[bA][:], sg, MUL, [tb[bA], ts_], [tAT])

        s4(0)
        for j in range(8):
            if j + 1 < 8:
                s4(j + 1)
            s5(j)
        dump("AT", AT, [128, 4 * 4096], BF16, tAT)
        S.barrier(); A.release(m_keep)
        if stop_after == "B":
            S.emit(fin)
            return nc, dbg_outs

        qT = A.alloc(4 * 4096, BF16)
        m_bt = A.mark()
        kT = A.alloc(4 * 4096, BF16)
        v_sb = A.alloc(32 * 512, BF16)
        tq = [[T("q%d_%d" % (h, j)) for j in range(NT)] for h in range(4)]
        tk = [T("k%d" % j) for j in range(NT)]
        tv = [T("v%d" % j) for j in range(NT)]
        qT3 = v3(qT, 4); kT3 = v3(kT, 4)
        m_qkv = A.mark()
        cosT = A.alloc(1024); sinT = A.alloc(1024); tR = T("rope")
        m_r = A.mark()
        ti_i = A.alloc(32).bitcast(I32); tf_ = A.alloc(32); ji_i = A.alloc(32).bitcast(I32); jf_ = A.alloc(32); frq = A.alloc(32)
        S.op("pool", lambda e: e.iota(ti_i, pattern=[[128, 32]], base=0, channel_multiplier=1), (), [tR])
        S.op("pool", lambda e: e.iota(ji_i, pattern=[[1, 32]], base=0, channel_multiplier=0), (), [tR])
        CP("dve", tf_, ti_i, [tR], [tR]); CP("dve", jf_, ji_i, [tR], [tR])
        ACTF(frq, jf_, AF.Exp, [tR], [tR], scale=-math.log(10000.0) / 32.0)
        ang = A.alloc(1024)
        TT("dve", v3(ang, 32), tf_.unsqueeze(2).to_broadcast([128, 32, 32]), frq.unsqueeze(1).to_broadcast([128, 32, 32]),
           MUL, [tR], [tR])
        sn_, cs_ = sincos(ang, 1024, tR)
        CP("dve", sinT, sn_, [tR], [tR]); CP("dve", cosT, cs_, [tR], [tR])
        S.barrier(); A.release(m_r)
        cosT3 = v3(cosT, 32); sinT3 = v3(sinT, 32)

        cast_weights()
        wqkv = A.alloc(8 * 1536, BF16); twq = T("wqkv")
        for kt2 in range(4):
            S.dma("sp", v3(wqkv, 8)[:, 2 * kt2:2 * kt2 + 2, :],
                  s_qkv[kt2 * 256:(kt2 + 1) * 256, :].rearrange("(kt p) c -> p kt c", p=128), reads=[tSqkv], writes=[twq])
        xbf = A.alloc(4096, BF16); txbf = T("xbf")
        xTs = [A.alloc(4096, BF16) for _ in range(2)]; txTs = [T("xT0"), T("xT1")]
        qrs = [A.alloc(512, BF16) for _ in range(2)]; tqr = [T("qr0"), T("qr1")]
        rw = [[A.alloc(256) for _ in range(4)] for _ in range(2)]; trw = [T("rw0"), T("rw1")]
        def qk_proj(j, blk, which, cnt):
            xT = xTs[j % 2]; txT = txTs[j % 2]
            gb = j * 4 + blk
            cb = cosT3[:, gb, :].unsqueeze(1).to_broadcast([128, 8, 32])
            sb_ = sinT3[:, gb, :].unsqueeze(1).to_broadcast([128, 8, 32])
            bi = nb()
            for kt in range(8):
                MM(banks[bi][:], xT[:, kt * 512 + blk * 128: kt * 512 + (blk + 1) * 128],
                   wqkv[:, kt * 1536 + which * 512: kt * 1536 + (which + 1) * 512], kt == 0, kt == 7,
                   [txT, twq], [tb[bi]])
            b4 = banks[bi][:].rearrange("p (g t d) -> p g t d", g=8, t=2)
            t1 = b4[:, :, 0, :]; t2 = b4[:, :, 1, :]
            w = rw[cnt % 2]; tw = trw[cnt % 2]
            qr = qrs[cnt % 2]; tqr_ = tqr[cnt % 2]
            qr4 = qr.rearrange("p (g t d) -> p g t d", g=8, t=2)
            TT("dve", v3(w[0], 8), t1, cb, MUL, [tb[bi], tR], [tw]); TT("dve", v3(w[1], 8), t2, sb_, MUL, [tb[bi], tR], [tw])
            TT("dve", v3(w[2], 8), t2, cb, MUL, [tb[bi], tR], [tw]); TT("dve", v3(w[3], 8), t1, sb_, MUL, [tb[bi], tR], [tw])
            TT("dve", qr4[:, :, 0, :], v3(w[0], 8), v3(w[1], 8), SUB, [tw], [tqr_])
            TT("dve", qr4[:, :, 1, :], v3(w[2], 8), v3(w[3], 8), ADD, [tw], [tqr_])

        def qk_trans(j, blk, which, cnt):
            tok0 = j * 512 + blk * 128
            qr = qrs[cnt % 2]; tqr_ = tqr[cnt % 2]
            bi2 = nb()
            pb = banks[bi2][:].bitcast(BF16)
            for h in range(4):
                TR(pb[:, h * 128:(h + 1) * 128], qr[:, h * 128:(h + 1) * 128], ident, [tqr_, tC], [tb[bi2]])
            if which == 0:
                CP("act", qT3[:, :, tok0:tok0 + 128], v3(pb[:, 0:512], 4), [tb[bi2]], [tq[h][j] for h in range(4)])
            else:
                CP("act", kT3[:, :, tok0:tok0 + 128], v3(pb[:, 0:512], 4), [tb[bi2]], [tk[j]])

        def v_proj(j, blk):
            xT = xTs[j % 2]; txT = txTs[j % 2]
            gb = j * 4 + blk
            bi = nb()
            for kt in range(8):
                MM(banks[bi][:], xT[:, kt * 512 + blk * 128: kt * 512 + (blk + 1) * 128],
                   wqkv[:, kt * 1536 + 1024: kt * 1536 + 1536], kt == 0, kt == 7, [txT, twq], [tb[bi]])
            CP("act", v_sb[:, gb * 512:(gb + 1) * 512], banks[bi][:], [tb[bi]], [tv[j]])

        cnt = 0
        prev = None
        load_xT(0, xbf, txbf, xTs[0], txTs[0])
        for j in range(NT):
            if j + 1 < NT:
                load_xT(j + 1, xbf, txbf, xTs[(j + 1) % 2], txTs[(j + 1) % 2])
            for blk in range(4):
                for which in range(2):
                    qk_proj(j, blk, which, cnt)
                    if prev is not None:
                        qk_trans(*prev)
                    prev = (j, blk, which, cnt)
                    cnt += 1
                v_proj(j, blk)
        qk_trans(*prev)
        dump("qT", qT, [128, 4 * 4096], BF16, tq[0][0])
        dump("kT", kT, [128, 4 * 4096], BF16, tk[0])
        dump("v", v_sb, [128, 32 * 512], BF16, tv[0])
        S.barrier(); A.release(m_qkv)
        if stop_after == "A_qkv":
            S.emit(fin)
            return nc, dbg_outs

        Pb = [[A.alloc(512, BF16) for _ in range(2)] for _ in range(3)]
        tPb = [[T("P%d_%d" % (s_, c)) for c in range(2)] for s_ in range(3)]
        NE = 3
        ew = [[A.alloc(512) for _ in range(6)] for _ in range(NE)]; tew = [T("ew%d" % i) for i in range(NE)]
        esq = [A.alloc(512, BF16) for _ in range(NE)]
        accs = [[A.alloc(512) for _ in range(2)] for _ in range(2)]
        tacc = [[T("acc%d_%d" % (a_, c)) for c in range(2)] for a_ in range(2)]
        ones_f = A.alloc(128); tof = T("ones_f")
        MS("dve", ones_f, 1.0, [tof])
        its = [(h, jq, kb) for h in range(4) for jq in range(NT) for kb in range(4 * jq + 4)]
        OB = 6

        def att_S(idx):
            h, jq, kb = its[idx]
            n0 = max(kb - 4 * jq, 0) * 128
            ks = slice(kb * 128, (kb + 1) * 128)
            for c in range(2):
                bS = 2 * (idx % 3) + c
                rs = slice(c * 64, (c + 1) * 64)
                MM(banks[bS][:, n0:512], kT3[rs, h, ks], qT3[rs, h, jq * 512 + n0:(jq + 1) * 512], True, True,
                   [tk[kb // 4], tq[h][jq]], [tb[bS]], tile_position=(c * 64, 0))

        def att_P(idx):
            h, jq, kb = its[idx]
            m = kb - 4 * jq
            n0 = max(m, 0) * 128
            for c in range(2):
                bS = 2 * (idx % 3) + c
                P = Pb[idx % 3][c]; tp = tPb[idx % 3][c]
                if n0 > 0:
                    MS("pool", P[:, 0:n0], 0.0, [tp])
                ACTF(P[:, n0:512], banks[bS][:, n0:512], AF.Exp, [tb[bS]], [tp], scale=0.125)
                if m >= 0:
                    TT("pool", P[:, n0:n0 + 128], P[:, n0:n0 + 128], tri, MUL, [tp, tC], [tp])

        def att_PV(idx):
            h, jq, kb = its[idx]
            nkb = 4 * jq + 4
            a_ = (h * NT + jq) % 2
            for c in range(2):
                P = Pb[idx % 3][c]; tp = tPb[idx % 3][c]
                MM(banks[OB + c][:], v_sb[:, kb * 512 + h * 128: kb * 512 + (h + 1) * 128], P, kb == 0, kb == nkb - 1,
                   [tv[kb // 4], tp], [tb[OB + c]])
            for c in range(2):
                P = Pb[idx % 3][c]; tp = tPb[idx % 3][c]
                eng = "dve" if c == 0 else "pool"
                acc = accs[a_][c]; ta = tacc[a_][c]
                if kb == 0:
                    CP(eng, acc, P, [tp], [ta])
                else:
                    TT(eng, acc, acc, P, ADD, [tp, ta], [ta])

        def att_epi_a(h, jq):
            e_ = (h * NT + jq) % NE
            w = ew[e_]; tw = tew[e_]
            CP("dve", w[2], banks[OB][:], [tb[OB]], [tw])
            CP("dve", w[3], banks[OB + 1][:], [tb[OB + 1]], [tw])

        def att_epi_b(h, jq, bank):
            e_ = (h * NT + jq) % NE
            a_ = (h * NT + jq) % 2
            w = ew[e_]; tw = tew[e_]; sq = esq[e_]
            for c in range(2):
                MM(banks[bank + c][:], ones_f, accs[a_][c], True, True, [tof, tacc[a_][c]], [tb[bank + c]])
            S.op("dve", lambda e, o=w[0], i_=banks[bank][:]: e.reciprocal(out=o, in_=i_), [tb[bank]], [tw])
            S.op("dve", lambda e, o=w[1], i_=banks[bank + 1][:]: e.reciprocal(out=o, in_=i_), [tb[bank + 1]], [tw])
            TT("dve", w[2], w[2], w[0], MUL, [tw], [tw])
            TT("dve", w[3], w[3], w[1], MUL, [tw], [tw])
            STT("dve", w[2], w[3], nlam_c, w[2], MUL, ADD, [tw, tC], [tw])
            TT("pool", sq, w[2], w[2], MUL, [tw], [tw])

        def att_epi_c(h, jq, bank):
            e_ = (h * NT + jq) % NE
            w = ew[e_]; tw = tew[e_]; sq = esq[e_]
            qs = slice(jq * 512, (jq + 1) * 512)
            MM(banks[bank][:], onesm_bf, sq, True, True, [tC, tw], [tb[bank]])
            ACTF(w[4], banks[bank][:], AF.Ln, [tb[bank], tC], [tw], bias=eps_c)
            ACTF(w[5], w[4], AF.Exp, [tw], [tw], scale=-0.5)
            STT("dve", qT3[:, h, qs], w[2], subln_c, w[5], MUL, MUL, [tw, tC, tCd], [tq[h][jq]])

        pend_b = []
        pend_c = []
        att_S(0)
        att_S(1)
        for idx in range(len(its)):
            if idx + 2 < len(its):
                att_S(idx + 2)
            att_P(idx)
            att_PV(idx)
            h, jq, kb = its[idx]
            fb = 2 * (idx % 3)
            while pend_b and pend_b[0][0] <= idx:
                _, ph, pjq = pend_b.pop(0)
                att_epi_b(ph, pjq, fb)
                pend_c.append((idx + 8, ph, pjq))
            while pend_c and pend_c[0][0] <= idx:
                _, ph, pjq = pend_c.pop(0)
                att_epi_c(ph, pjq, fb)
            if kb == 4 * jq + 3:
                att_epi_a(h, jq)
                pend_b.append((idx + 1, h, jq))
        for (_, ph, pjq) in pend_b:
            att_epi_b(ph, pjq, 0)
            pend_c.append((0, ph, pjq))
        for (_, ph, pjq) in pend_c:
            att_epi_c(ph, pjq, 2)
        dump("BT", qT, [128, 4 * 4096], BF16, tq[3][7])
        S.barrier(); A.release(m_bt)
        if stop_after == "C":
            S.emit(fin)
            return nc, dbg_outs

        BT3 = qT3
        lnp = {}
        for nm, src in (("g1", ln1g), ("b1", ln1b), ("g2", ln2g), ("b2", ln2b)):
            lnp[nm] = A.alloc(DM)
            S.dma("sp", lnp[nm], src.to_broadcast([128, DM]), writes=[tC])
        hT = A.alloc(22 * 512, BF16); thT = T("hT")
        xT = A.alloc(4096, BF16); txT = T("xT")
        merged = A.alloc(4096, BF16); tmg = T("merged")
        xbf = merged
        x1 = A.alloc(4 * 1024); tx1 = [T("x1_%d" % b) for b in range(4)]
        x1bf = [A.alloc(1024, BF16) for _ in range(2)]; tx1bf = [T("x1bf0"), T("x1bf1")]
        gtmp = [[A.alloc(512) for _ in range(2)] for _ in range(2)]; tgt = [T("gt0"), T("gt1")]
        ftmp = [[A.alloc(512) for _ in range(2)] for _ in range(2)]; tft = [T("ft0"), T("ft1")]
        ysb = [A.alloc(1024) for _ in range(2)]; tys = [T("ys0"), T("ys1")]
        xblk = [A.alloc(1024) for _ in range(2)]; txb = [T("xb0"), T("xb1")]
        lnw = [A.alloc(20) for _ in range(2)]; tlnw = [T("lnw0"), T("lnw1")]
        NSLOT = 6
        slots = [A.alloc(2048, BF16) for _ in range(NSLOT)]; tslot = [T("slot%d" % i) for i in range(NSLOT)]
        slot_i = [0]

        def wload(src, a, tsrc):
            i = slot_i[0] % NSLOT
            slot_i[0] += 1
            view = v3(slots[i], a)
            S.dma("sp", view, src, reads=[tsrc], writes=[tslot[i]])
            return view, tslot[i]

        def ln_resid(src_res, tres, bank0, ys, ty):
            for half in range(2):
                hs = slice(half * 512, (half + 1) * 512)
                STT("dve", ys[:, hs], src_res[:, hs], ALPHA, banks[bank0 + half][:], MUL, ADD, [tres, tb[bank0 + half]], [ty])

        def layer_norm(blk, lni, src_res, tres, bank0, gname, bname, dst, tdst, ys=None, ty=None):
            lw = lnw[lni % 2]; tl = tlnw[lni % 2]
            if ys is None:
                ys = ysb[lni % 2]; ty = tys[lni % 2]
                ln_resid(src_res, tres, bank0, ys, ty)
            for half in range(2):
                S.op("dve", lambda e, o=lw[:, half * 6:(half + 1) * 6], i_=ys[:, half * 512:(half + 1) * 512]: e.bn_stats(out=o, in_=i_),
                     [ty], [tl])
            S.op("dve", lambda e, o=lw[:, 12:14], i_=v3(lw[:, 0:12], 2): e.bn_aggr(out=o, in_=i_), [tl], [tl])
            ACTF(lw[:, 14:15], lw[:, 13:14], AF.Sqrt, [tl, tC], [tl], bias=eps_c)
            S.op("dve", lambda e, o=lw[:, 15:16], i_=lw[:, 14:15]: e.reciprocal(out=o, in_=i_), [tl], [tl])
            STT("dve", ys, ys, lw[:, 12:13], lnp[gname], SUB, MUL, [ty, tl, tC], [ty])
            STT("dve", dst, ys, lw[:, 15:16], lnp[bname], MUL, ADD, [ty, tl, tC], [tdst])

        lni = 0
        gcnt = 0
        fcnt = 0
        pend_stores = []
        pend_ln2 = []
        lni2 = [1000]

        def ln2_rest(jj, blk):
            x1b = x1[:, blk * 1024:(blk + 1) * 1024]
            layer_norm(blk, lni2[0], None, None, None, "g2", "b2", x1b, tx1[blk], ys=x1b, ty=tx1[blk]); lni2[0] += 1
            pend_stores.append((out[jj * 512 + blk * 128: jj * 512 + (blk + 1) * 128, :], x1b, tx1[blk]))

        def flush_stores():
            while pend_stores:
                dst_, src_, t_ = pend_stores.pop(0)
                fin.append(S.dma("sp", dst_, src_, reads=[t_]))

        for j in range(NT):
            ts_ = slice(j * 512, (j + 1) * 512)
            if j == 0:
                load_xT(j, xbf, tmg, xT, txT)
            for dt_ in range(8):
                if dt_ % 2 == 0:
                    pu, tpu = wload(s_p[:, dt_ * 256: dt_ * 256 + 512].rearrange("(kt p) c -> p kt c", p=128), 4, tSp)
                gu, tgu = wload(s_g[:, dt_ * 256:(dt_ + 1) * 256].rearrange("(kt p) c -> p kt c", p=128), 8, tSg)
                bg_, bp_ = [nb(), nb()], [nb(), nb()]
                for br in range(2):
                    for kt in range(8):
                        MM(banks[bg_[br]][:], gu[:, kt, br * 128:(br + 1) * 128], xT[:, kt * 512:(kt + 1) * 512], kt == 0, kt == 7,
                           [tgu, txT], [tb[bg_[br]]])
                for br in range(2):
                    for kt in range(4):
                        if br == 0:
                            rhs = AT[:, kt * 4096 + j * 512: kt * 4096 + (j + 1) * 512]; tr_ = tAT
                        else:
                            rhs = BT3[:, kt, ts_]; tr_ = tq[kt][j]
                        c0 = (dt_ % 2) * 256 + br * 128
                        MM(banks[bp_[br]][:], pu[:, kt, c0:c0 + 128], rhs, kt == 0, kt == 3, [tpu, tr_], [tb[bp_[br]]])
                gw = gtmp[gcnt % 2]; tg_ = tgt[gcnt % 2]; gcnt += 1
                for br in range(2):
                    ACTF(gw[br], banks[bg_[br]][:], AF.Sigmoid, [tb[bg_[br]], tC, tCd], [tg_], bias=bg_sb[:, br * 8 + dt_: br * 8 + dt_ + 1])
                for br in range(2):
                    TT("dve", gw[br], banks[bp_[br]][:], gw[br], MUL, [tb[bp_[br]], tg_], [tg_])
                TT("pool", merged[:, dt_ * 512:(dt_ + 1) * 512], gw[0], gw[1], ADD, [tg_], [tmg])
                if dt_ % 2 == 1 and pend_ln2:
                    ln2_rest(*pend_ln2.pop(0))
            flush_stores()
            wos = [wload(s_o[u * 256:(u + 1) * 256, :].rearrange("(k p) c -> p k c", p=128), 2, tSo) for u in range(4)]
            def mix_mm(blk):
                for half in range(2):
                    bk = blk * 2 + half
                    for kt in range(8):
                        wo, two = wos[kt // 2]
                        MM(banks[bk][:], merged[:, kt * 512 + blk * 128: kt * 512 + (blk + 1) * 128],
                           wo[:, kt % 2, half * 512:(half + 1) * 512], kt == 0, kt == 7, [tmg, two], [tb[bk]])
            mix_mm(0)
            for blk in range(4):
                if blk + 1 < 4:
                    mix_mm(blk + 1)
                xb = xblk[blk % 2]; txb_ = txb[blk % 2]
                S.dma("sp", xb, x[j * 512 + blk * 128: j * 512 + (blk + 1) * 128, :], writes=[txb_])
                x1b = x1[:, blk * 1024:(blk + 1) * 1024]
                layer_norm(blk, lni, xb, txb_, blk * 2, "g1", "b1", x1b, tx1[blk]); lni += 1
                xh = x1bf[blk % 2]; txh = tx1bf[blk % 2]
                CP("act", xh, x1b, [tx1[blk]], [txh])
                bi = blk * 2
                pb = banks[bi][:].bitcast(BF16)
                for dt_ in range(8):
                    TR(pb[:, dt_ * 128:(dt_ + 1) * 128], xh[:, dt_ * 128:(dt_ + 1) * 128], ident, [txh, tC], [tb[bi]])
                CP("dve", v3(xT, 8)[:, :, blk * 128:(blk + 1) * 128], v3(pb, 8), [tb[bi]], [txT])
            for ot2 in range(11):
                gw_, tgw_ = wload(s_fg[:, ot2 * 256:(ot2 + 1) * 256].rearrange("(kt p) c -> p kt c", p=128), 8, tSfg)
                uw_, tuw_ = wload(s_fu[:, ot2 * 256:(ot2 + 1) * 256].rearrange("(kt p) c -> p kt c", p=128), 8, tSfu)
                for otl in range(2):
                    ot = ot2 * 2 + otl
                    ba, bb_ = nb(), nb()
                    for kt in range(8):
                        MM(banks[ba][:], gw_[:, kt, otl * 128:(otl + 1) * 128], xT[:, kt * 512:(kt + 1) * 512], kt == 0, kt == 7,
                           [tgw_, txT], [tb[ba]])
                    for kt in range(8):
                        MM(banks[bb_][:], uw_[:, kt, otl * 128:(otl + 1) * 128], xT[:, kt * 512:(kt + 1) * 512], kt == 0, kt == 7,
                           [tuw_, txT], [tb[bb_]])
                    fw_ = ftmp[fcnt % 2]; tf__ = tft[fcnt % 2]; fcnt += 1
                    ACTF(fw_[0], banks[ba][:], AF.Sigmoid, [tb[ba]], [tf__])
                    TT("dve", fw_[1], banks[ba][:], fw_[0], MUL, [tb[ba], tf__], [tf__])
                    TT("dve", hT[:, ot * 512:(ot + 1) * 512], banks[bb_][:], fw_[1], MUL, [tb[bb_], tf__], [thT])
            if j + 1 < NT:
                load_xT(j + 1, xbf, tmg, xT, txT)
            for u in range(11):
                dw, tdw = wload(s_fd[u * 256:(u + 1) * 256, :].rearrange("(k p) c -> p k c", p=128), 2, tSfd)
                for kl in range(2):
                    kt = 2 * u + kl
                    for blk in range(4):
                        for half in range(2):
                            bk = blk * 2 + half
                            MM(banks[bk][:], hT[:, kt * 512 + blk * 128: kt * 512 + (blk + 1) * 128],
                               dw[:, kl, half * 512:(half + 1) * 512], kt == 0, kt == 21, [thT, tdw], [tb[bk]])
            for blk in range(4):
                x1b = x1[:, blk * 1024:(blk + 1) * 1024]
                ln_resid(x1b, tx1[blk], blk * 2, x1b, tx1[blk])
            for blk in range(4):
                pend_ln2.append((j, blk))
            if j == NT - 1:
                while pend_ln2:
                    ln2_rest(*pend_ln2.pop(0))
        flush_stores()
        S.emit(fin)
    print("arena peak words", A.peak, "of", A.words, " ops", len(S.ops), flush=True)
    return nc, dbg_outs


def prep_common(inp):
    f = lambda a: np.ascontiguousarray(np.asarray(a, dtype=np.float32))
    c = {}
    c["w_in"] = f(inp["w_in"][0])
    c["bgate"] = f(inp["b_gate"][0].reshape(16, 128).T)
    lre, lim = inp["ssm_lambda_re"][0], inp["ssm_lambda_im"][0]
    lst = np.broadcast_to(inp["ssm_log_step"][0][:, None], (G, P_))
    bre, bim = inp["ssm_b_re"][0], inp["ssm_b_im"][0]
    cre, cim = inp["ssm_c_re"][0], inp["ssm_c_im"][0]
    dsk = inp["ssm_d"][0]

    def to_rl(a):
        a = np.asarray(a).reshape(4, 8, 1, P_)
        a = np.broadcast_to(a, (4, 8, H_, P_))
        return f(a.transpose(1, 2, 0, 3).reshape(128, 256))
    c["rl_lre"], c["rl_lim"], c["rl_lst"] = to_rl(lre), to_rl(lim), to_rl(lst)

    def b_rl(b):
        b = np.asarray(b).reshape(4, 8, P_, H_)
        return f(b.transpose(1, 3, 0, 2).reshape(128, 256))
    c["rl_bre"], c["rl_bim"] = b_rl(bre), b_rl(bim)
    c["d_rl"] = f(np.asarray(dsk).reshape(4, 8, H_).transpose(1, 2, 0).reshape(128, 4))

    def to_sl(a):
        a = np.asarray(a).reshape(16, 2, P_)
        return f(a.transpose(1, 2, 0).reshape(128, 16))
    c["sl_lre"], c["sl_lim"], c["sl_lst"] = to_sl(lre), to_sl(lim), to_sl(lst)

    def c_sl(cc):
        cc = np.asarray(cc).reshape(16, 2, H_, P_)
        return f(cc.transpose(1, 3, 0, 2).reshape(128, 256))
    c["sl_cre"], c["sl_cim"] = c_sl(cre), c_sl(cim)

    def to_gl(a):
        a = np.asarray(a).T
        return f(np.concatenate([a, a], axis=0))
    c["gl_lre"], c["gl_lim"], c["gl_lst"] = to_gl(lre), to_gl(lim), to_gl(lst)
    bre_t = np.asarray(bre).transpose(1, 0, 2).reshape(P_, G * H_)
    bim_t = np.asarray(bim).transpose(1, 0, 2).reshape(P_, G * H_)
    c["gl_xb"] = f(np.concatenate([bre_t, bim_t], axis=0))
    c["gl_yb"] = f(np.concatenate([bim_t, bre_t], axis=0))
    cre_t = np.asarray(cre).transpose(2, 0, 1).reshape(P_, G * H_)
    cim_t = np.asarray(cim).transpose(2, 0, 1).reshape(P_, G * H_)
    c["gl_cc"] = f(np.concatenate([cre_t, cim_t], axis=0))
    c["w_glu"] = f(inp["w_glu"][0])
    c["lamq1"] = f(inp["lambda_q1"][0][None, :]); c["lamk1"] = f(inp["lambda_k1"][0][None, :])
    c["lamq2"] = f(inp["lambda_q2"][0][None, :]); c["lamk2"] = f(inp["lambda_k2"][0][None, :])
    c["subln"] = f(inp["subln_gain"][0][:, None])
    c["w_ps"] = f(inp["w_proj_ssm"][0]); c["w_pa"] = f(inp["w_proj_attn"][0]); c["w_out"] = f(inp["w_out"][0])
    c["ln1g"] = f(inp["ln1_g"][0][None, :]); c["ln1b"] = f(inp["ln1_b"][0][None, :])
    c["ln2g"] = f(inp["ln2_g"][0][None, :]); c["ln2b"] = f(inp["ln2_b"][0][None, :])
    c["w_fg"] = f(inp["w_ffn_gate"][0]); c["w_fu"] = f(inp["w_ffn_up"][0]); c["w_fd"] = f(inp["w_ffn_down"][0])
    return c


_NC_CACHE = {}


def kernel(**inputs):
    if "nc" not in _NC_CACHE:
        _NC_CACHE["nc"] = build()[0]
    nc = _NC_CACHE["nc"]
    common = prep_common(inputs)
    xs = np.asarray(inputs["x"], dtype=np.float32)
    in_maps = []
    for b in range(8):
        m = dict(common)
        m["x"] = np.ascontiguousarray(xs[b])
        in_maps.append(m)
    res = run_bass_kernel_spmd(nc, in_maps, core_ids=list(range(8)))
    return np.stack([np.asarray(r["out"], dtype=np.float32) for r in res.results], axis=0)
```

```python
import math
import contextlib
import numpy as np
import concourse.bass as bass
import concourse.mybir as mybir
from concourse.bass_utils import run_bass_kernel_spmd

F32 = mybir.dt.float32
BF16 = mybir.dt.bfloat16
I32 = mybir.dt.int32
AF = mybir.ActivationFunctionType
ALU = mybir.AluOpType

SEQ = 4096
DM = 1024
DFF = 2816
NT = 8
G = 32
P_ = 64
H_ = 16
ALPHA = 2.0 ** 0.25
LAMBDA_INIT = 0.8 - 0.6 * math.exp(0.0)
TWO_PI_LO = 6.283185
INV_2PI = 1.0 / (2.0 * math.pi)


class T:
    __slots__ = ("name", "w", "r")

    def __init__(self, name):
        self.name = name
        self.w = {}
        self.r = {}


class Sched:
    ENGS = ("pe", "act", "dve", "pool", "sp")

    def __init__(self, nc, n_dma_sems=48):
        self.nc = nc
        self.ops = []
        self.n_dma_sems = n_dma_sems
        self.dma_rr = 0
        self.dma_last = {}
        self.last = {}
        self.n_sw = 0
        self.bg = set()

    def _deps(self, key, reads, writes):
        deps = set()
        for t in reads:
            for k, o in t.w.items():
                deps.add(o)
        for t in writes:
            for k, o in t.w.items():
                deps.add(o)
            for k, o in t.r.items():
                deps.add(o)
        return deps

    def op(self, eng, fn, reads=(), writes=()):
        oid = len(self.ops)
        deps = self._deps(eng, reads, writes)
        if eng == "pe":
            deps = {d for d in deps if self.ops[d][0] != "pe" or self.ops[d][3] is not None}
        self.ops.append((eng, fn, deps, None))
        for t in reads:
            t.r[eng] = oid
        for t in writes:
            t.w[eng] = oid
        self.last[eng] = oid
        return oid

    def dma(self, eng, out, in_, reads=(), writes=(), **kw):
        oid = len(self.ops)
        if eng == "pool":
            s = ("sw", self.n_sw)
            self.n_sw += 1
        else:
            s = self.dma_rr
            self.dma_rr = (self.dma_rr + 1) % self.n_dma_sems
        key = ("dma", s)
        deps = self._deps(key, reads, writes)
        if s in self.dma_last:
            deps.add(self.dma_last[s])
        self.dma_last[s] = oid

        kw2 = {k: v for k, v in kw.items() if k != "_bg"}

        def fn(e, out=out, in_=in_, kw=kw2):
            return e.dma_start(out=out, in_=in_, **kw)
        self.ops.append((eng, fn, deps, s))
        if kw.get("_bg"):
            self.bg.add(oid)
        for t in reads:
            t.r[key] = oid
        for t in writes:
            t.w[key] = oid
        return oid

    def barrier(self):
        deps = set(self.last.values()) | set(v for v in self.dma_last.values() if v not in self.bg)
        for eng in self.ENGS:
            self.ops.append((eng, None, set(deps), None))

    def emit(self, final_wait_ops=()):
        nc = self.nc
        ops = self.ops
        n = len(ops)
        needed = [False] * n
        for (eng, fn, deps, ds) in ops:
            for d in deps:
                needed[d] = True
        for d in final_wait_ops:
            needed[d] = True
        semval = [None] * n
        cnt = {}
        for i, (eng, fn, deps, ds) in enumerate(ops):
            if ds is not None:
                key = ("dma", ds)
                cnt[key] = cnt.get(key, 0) + 16
                semval[i] = (key, cnt[key])
            elif needed[i]:
                cnt[eng] = cnt.get(eng, 0) + 1
                semval[i] = (eng, cnt[eng])
        with contextlib.ExitStack() as st:
            sems = {}
            for e in self.ENGS:
                sems[e] = st.enter_context(nc.semaphore("s_" + e))
            for s in range(self.n_dma_sems):
                sems[("dma", s)] = st.enter_context(nc.semaphore("s_dma%d" % s))
            for s in range(self.n_sw):
                sems[("dma", ("sw", s))] = st.enter_context(nc.semaphore("s_sw%d" % s))
            block = st.enter_context(nc.Block())
            seen = {e: {} for e in self.ENGS}
            per_eng = {e: [] for e in self.ENGS}
            for i, (eng, fn, deps, ds) in enumerate(ops):
                waits = {}
                for d in deps:
                    k, v = semval[d]
                    if waits.get(k, 0) < v:
                        waits[k] = v
                wl = []
                for k, v in waits.items():
                    if k == eng and fn is None:
                        continue
                    if seen[eng].get(k, 0) < v:
                        seen[eng][k] = v
                        wl.append((k, v))
                per_eng[eng].append((wl, fn, semval[i]))
            fw = {}
            for d in final_wait_ops:
                k, v = semval[d]
                if fw.get(k, 0) < v:
                    fw[k] = v

            def run(e, lst, extra=None):
                for (wl, fn, sv) in lst:
                    for (k, v) in wl:
                        e.wait_ge(sems[k], v)
                    if fn is None:
                        continue
                    ins = fn(e)
                    if sv is not None:
                        k, v = sv
                        ins.then_inc(sems[k], 16 if isinstance(k, tuple) else 1)
                if extra:
                    for k, v in extra.items():
                        e.wait_ge(sems[k], v)

            @block.tensor
            def _(e):
                run(e, per_eng["pe"])

            @block.scalar
            def _(e):
                run(e, per_eng["act"])

            @block.vector
            def _(e):
                run(e, per_eng["dve"])

            @block.gpsimd
            def _(e):
                run(e, per_eng["pool"])

            @block.sync
            def _(e):
                run(e, per_eng["sp"], fw)


class Arena:
    def __init__(self, ap, words):
        self.ap = ap
        self.words = words
        self.pos = 0
        self.peak = 0

    def alloc(self, n, dt=F32):
        if dt == BF16:
            w = (n + 1) // 2
        else:
            w = n
        assert self.pos + w <= self.words, ("arena overflow", self.pos, w, self.words)
        v = self.ap[:, self.pos:self.pos + w]
        self.pos += w
        self.peak = max(self.peak, self.pos)
        if dt != F32:
            v = v.bitcast(dt)
        return v

    def mark(self):
        return self.pos

    def release(self, m):
        self.pos = m


def v3(ap, a):
    return ap.rearrange("p (a b) -> p a b", a=a)


def build(stop_after=None, dbg=()):
    nc = bass.Bass("TRN2", target_bir_lowering=False)

    def din(name, shape, dt=F32):
        return nc.dram_tensor(name, list(shape), dt, kind="ExternalInput").ap()

    x = din("x", [SEQ, DM])
    w_in = din("w_in", [DM, 4096])
    bgate = din("bgate", [128, 16])
    rl = {k: din("rl_" + k, [128, 256]) for k in ("lre", "lim", "lst", "bre", "bim")}
    d_rl = din("d_rl", [128, 4])
    sl = {k: din("sl_" + k, [128, 16]) for k in ("lre", "lim", "lst")}
    sl_cre = din("sl_cre", [128, 256])
    sl_cim = din("sl_cim", [128, 256])
    gl = {k: din("gl_" + k, [128, 32]) for k in ("lre", "lim", "lst")}
    gl_xb = din("gl_xb", [128, 512])
    gl_yb = din("gl_yb", [128, 512])
    gl_cc = din("gl_cc", [128, 512])
    w_glu = din("w_glu", [512, 1024])
    lamq1 = din("lamq1", [1, 64]); lamk1 = din("lamk1", [1, 64])
    lamq2 = din("lamq2", [1, 64]); lamk2 = din("lamk2", [1, 64])
    subln = din("subln", [128, 1])
    w_ps = din("w_ps", [512, DM])
    w_pa = din("w_pa", [512, DM])
    w_out = din("w_out", [DM, DM])
    ln1g = din("ln1g", [1, DM]); ln1b = din("ln1b", [1, DM])
    ln2g = din("ln2g", [1, DM]); ln2b = din("ln2b", [1, DM])
    w_fg = din("w_fg", [DM, DFF]); w_fu = din("w_fu", [DM, DFF]); w_fd = din("w_fd", [DFF, DM])
    out = nc.dram_tensor("out", [SEQ, DM], F32, kind="ExternalOutput").ap()

    def dscr(name, shape):
        return nc.dram_tensor(name, list(shape), BF16, kind="Internal").ap()
    s_x = dscr("s_x", [SEQ, DM])
    s_qkv = dscr("s_qkv", [DM, 1536])
    s_glu = dscr("s_glu", [512, 1024])
    s_g = dscr("s_g", [DM, 2048])
    s_p = dscr("s_p", [512, 2048])
    s_o = dscr("s_o", [DM, DM])
    s_fg = dscr("s_fg", [DM, DFF]); s_fu = dscr("s_fu", [DM, DFF]); s_fd = dscr("s_fd", [DFF, DM])

    dbg_outs = {}

    S = Sched(nc)
    fin = []
    st = contextlib.ExitStack()
    with st:
        AW = 53100
        arena_t = st.enter_context(nc.sbuf_tensor("arena", [128, AW], F32))
        A = Arena(arena_t, AW)
        banks = [st.enter_context(nc.psum_tensor("bank%d" % i, [128, 512], F32)) for i in range(8)]
        tb = [T("bank%d" % i) for i in range(8)]
        rr = [0]

        def nb():
            i = rr[0]
            rr[0] = (rr[0] + 1) % 8
            return i

        def MM(o, lhsT, rhs, start, stop, reads, writes, **kw):
            S.op("pe", lambda e: e.matmul(o, lhsT=lhsT, rhs=rhs, start=start, stop=stop, **kw), reads, writes)

        def TR(o, in_, ident, reads, writes):
            S.op("pe", lambda e: e.transpose(o, in_, ident), reads, writes)

        def TT(eng, o, a, b, op, reads, writes):
            S.op(eng, lambda e: e.tensor_tensor(out=o, in0=a, in1=b, op=op), reads, writes)

        def TS(eng, o, a, s1, s2, op0, op1, reads, writes):
            if s2 is None:
                S.op(eng, lambda e: e.tensor_scalar(out=o, in0=a, scalar1=s1, scalar2=None, op0=op0), reads, writes)
            else:
                S.op(eng, lambda e: e.tensor_scalar(out=o, in0=a, scalar1=s1, scalar2=s2, op0=op0, op1=op1), reads, writes)

        def STT(eng, o, a, s, b, op0, op1, reads, writes):
            S.op(eng, lambda e: e.scalar_tensor_tensor(out=o, in0=a, scalar=s, in1=b, op0=op0, op1=op1), reads, writes)

        def ACTF(o, in_, func, reads, writes, bias=None, scale=1.0):
            if bias is None:
                S.op("act", lambda e: e.activation(out=o, in_=in_, func=func, scale=scale), reads, writes)
            else:
                S.op("act", lambda e: e.activation(out=o, in_=in_, func=func, bias=bias, scale=scale), reads, writes)

        def CP(eng, o, in_, reads, writes):
            if eng == "act":
                S.op("act", lambda e: e.copy(out=o, in_=in_), reads, writes)
            else:
                S.op(eng, lambda e: e.tensor_copy(out=o, in_=in_), reads, writes)

        def MS(eng, o, val, writes):
            S.op(eng, lambda e: e.memset(o, val), (), writes)

        def dump(name, ap, shape, dt, t):
            if name in dbg:
                S.barrier()
                d = nc.dram_tensor("dbg_" + name, list(shape), dt, kind="ExternalOutput").ap()
                dbg_outs[name] = (shape, dt)
                fin.append(S.dma("sp", d, ap, reads=[t]))

        tC = T("consts")
        ident = A.alloc(128, BF16)
        MS("pool", ident, 0.0, [tC])
        S.op("pool", lambda e: e.affine_select(out=ident, in_=ident, pattern=[[-1, 128]], compare_op=ALU.not_equal,
                                               fill=1.0, base=0, channel_multiplier=1), [tC], [tC])
        ones_bf = A.alloc(128, BF16)
        MS("pool", ones_bf, 1.0, [tC])
        onesm_bf = A.alloc(128, BF16)
        MS("pool", onesm_bf, 1.0 / 128.0, [tC])
        tri = A.alloc(128, BF16)
        MS("pool", tri, 1.0, [tC])
        S.op("pool", lambda e: e.affine_select(out=tri, in_=tri, pattern=[[1, 128]], compare_op=ALU.is_ge,
                                               fill=0.0, base=0, channel_multiplier=-1), [tC], [tC])
        zero_c = A.alloc(1); MS("dve", zero_c, 0.0, [tC])
        eps_c = A.alloc(1); MS("dve", eps_c, 1e-5, [tC])
        pidx_i = A.alloc(1).bitcast(I32)
        S.op("pool", lambda e: e.iota(pidx_i, pattern=[[0, 1]], base=0, channel_multiplier=1), (), [tC])
        tmp_i = A.alloc(1).bitcast(I32)
        odd_c = A.alloc(1)
        even_c = A.alloc(1)
        sgn_c = A.alloc(1)
        nsgn_c = A.alloc(1)
        rowg_c = A.alloc(1)
        S.op("dve", lambda e: e.tensor_single_scalar(out=tmp_i, in_=pidx_i, scalar=4, op=ALU.arith_shift_right), [tC], [tC])
        CP("dve", rowg_c, tmp_i, [tC], [tC])
        S.op("dve", lambda e: e.tensor_single_scalar(out=tmp_i, in_=tmp_i, scalar=1, op=ALU.bitwise_and), [tC], [tC])
        CP("dve", odd_c, tmp_i, [tC], [tC])
        TS("dve", even_c, odd_c, -1.0, 1.0, ALU.mult, ALU.add, [tC], [tC])
        S.op("dve", lambda e: e.tensor_single_scalar(out=tmp_i, in_=pidx_i, scalar=6, op=ALU.arith_shift_right), [tC], [tC])
        CP("dve", sgn_c, tmp_i, [tC], [tC])
        TS("dve", sgn_c, sgn_c, 2.0, -1.0, ALU.mult, ALU.add, [tC], [tC])
        TS("dve", nsgn_c, sgn_c, -1.0, None, ALU.mult, None, [tC], [tC])

        tCd = T("consts_dma")
        bg_sb = A.alloc(16)
        S.dma("sp", bg_sb, bgate, writes=[tCd])
        subln_c = A.alloc(1)
        S.dma("sp", subln_c, subln, writes=[tCd])
        lq = A.alloc(256)
        for i, src in enumerate((lamq1, lamk1, lamq2, lamk2)):
            S.dma("sp", lq[:, i * 64:(i + 1) * 64], src.to_broadcast([128, 64]), writes=[tCd])
        lsum = A.alloc(2)
        lprod = A.alloc(128)
        nlam_c = A.alloc(1)

        def late_consts():
            TS("dve", subln_c, subln_c, 1.0 - LAMBDA_INIT, None, ALU.mult, None, [tCd], [tCd])
            TT("dve", lprod[:, 0:64], lq[:, 0:64], lq[:, 64:128], ALU.mult, [tCd], [tCd])
            TT("dve", lprod[:, 64:128], lq[:, 128:192], lq[:, 192:256], ALU.mult, [tCd], [tCd])
            S.op("dve", lambda e: e.reduce_sum(out=lsum, in_=v3(lprod, 2), axis=mybir.AxisListType.X), [tCd], [tCd])
            ACTF(lsum, lsum, AF.Exp, [tCd], [tCd])
            TT("dve", nlam_c, lsum[:, 1:2], lsum[:, 0:1], ALU.subtract, [tCd], [tCd])
            TS("dve", nlam_c, nlam_c, -LAMBDA_INIT, None, ALU.add, None, [tCd], [tC])

        tSg, tSp, tSo, tSfg, tSfu, tSfd = (T("s_g"), T("s_p"), T("s_o"), T("s_fg"), T("s_fu"), T("s_fd"))

        def cast_weights():
            sg4 = s_g.rearrange("k (d b c) -> k d b c", d=8, b=2)
            for br in range(2):
                for r in range(2):
                    rs = slice(r * 512, (r + 1) * 512)
                    S.dma("pool", sg4[rs, :, br, :],
                          w_in[rs, 2048 + br * 1024: 2048 + (br + 1) * 1024].rearrange("k (d c) -> k d c", d=8), writes=[tSg], _bg=True)
            sp4 = s_p.rearrange("k (d b c) -> k d b c", d=8, b=2)
            S.dma("pool", sp4[:, :, 0, :], w_ps.rearrange("k (d c) -> k d c", d=8), writes=[tSp], _bg=True)
            S.dma("pool", sp4[:, :, 1, :], w_pa.rearrange("k (d c) -> k d c", d=8), writes=[tSp], _bg=True)
            for r in range(2):
                S.dma("pool", s_o[r * 512:(r + 1) * 512, :], w_out[r * 512:(r + 1) * 512, :], writes=[tSo], _bg=True)
            for r in range(2):
                S.dma("pool", s_fg[r * 512:(r + 1) * 512, :], w_fg[r * 512:(r + 1) * 512, :], writes=[tSfg], _bg=True)
                S.dma("pool", s_fu[r * 512:(r + 1) * 512, :], w_fu[r * 512:(r + 1) * 512, :], writes=[tSfu], _bg=True)
            for r in range(4):
                S.dma("pool", s_fd[r * 704:(r + 1) * 704, :], w_fd[r * 704:(r + 1) * 704, :], writes=[tSfd], _bg=True)

        AT = A.alloc(4 * 4096, BF16); tAT = T("AT")
        m_keep = A.mark()
        uT = A.alloc(4 * 4096, BF16); tuT = T("uT")
        m_persist = A.mark()

        wu = A.alloc(8 * 512, BF16); twu = T("wu")
        tSx = [T("s_x%d" % j) for j in range(NT)]
        tSqkv = T("s_qkv"); tSglu = T("s_glu")
        xst = [A.alloc(4096) for _ in range(2)]; txst = [T("xst0"), T("xst1")]
        S.dma("sp", v3(xst[1], 8), w_in[:, 0:512].rearrange("(kt p) c -> p kt c", p=128), writes=[txst[1]])
        CP("dve", wu[:, 0:2048], xst[1][:, 0:2048], [txst[1]], [twu])
        CP("act", wu[:, 2048:4096], xst[1][:, 2048:4096], [txst[1]], [twu])

        def load_xT(j, xbf, txbf, xT, txT, first_pass=False):
            if first_pass:
                st_ = xst[j % 2]; tst_ = txst[j % 2]
                S.dma("sp", v3(st_, 4), x[j * 512:(j + 1) * 512, :].rearrange("(b p) d -> p b d", p=128), writes=[tst_])
                CP("act", xbf[:, 0:1024], st_[:, 0:1024], [tst_], [txbf])
                CP("dve", xbf[:, 1024:2048], st_[:, 1024:2048], [tst_], [txbf])
                CP("act", xbf[:, 2048:3072], st_[:, 2048:3072], [tst_], [txbf])
                CP("dve", xbf[:, 3072:4096], st_[:, 3072:4096], [tst_], [txbf])
                S.dma("sp", s_x[j * 512:(j + 1) * 512, :].rearrange("(b p) d -> p b d", p=128), v3(xbf, 4),
                      reads=[txbf], writes=[tSx[j]])
            else:
                S.dma("sp", v3(xbf, 4), s_x[j * 512:(j + 1) * 512, :].rearrange("(b p) d -> p b d", p=128),
                      reads=[tSx[j]], writes=[txbf])
            for dt_ in range(8):
                bi = nb()
                pb = banks[bi][:].bitcast(BF16)
                for b in range(4):
                    TR(pb[:, b * 128:(b + 1) * 128], xbf[:, b * 1024 + dt_ * 128: b * 1024 + (dt_ + 1) * 128], ident,
                       [txbf, tC], [tb[bi]])
                CP("act" if dt_ % 2 == 0 else "dve", xT[:, dt_ * 512:(dt_ + 1) * 512], pb[:, 0:512], [tb[bi]], [txT])

        xbfs = [A.alloc(4096, BF16) for _ in range(2)]; txbfs = [T("xbf0"), T("xbf1")]
        xTs = [A.alloc(4096, BF16) for _ in range(2)]; txTs = [T("xT0"), T("xT1")]
        load_xT(0, xbfs[0], txbfs[0], xTs[0], txTs[0], first_pass=True)
        for j in range(NT):
            xbf, txbf, xT, txT = xbfs[j % 2], txbfs[j % 2], xTs[j % 2], txTs[j % 2]
            if j + 1 < NT:
                load_xT(j + 1, xbfs[(j + 1) % 2], txbfs[(j + 1) % 2], xTs[(j + 1) % 2], txTs[(j + 1) % 2], first_pass=True)
            for gt in range(4):
                bi = nb()
                for kt in range(8):
                    MM(banks[bi][:], wu[:, kt * 512 + gt * 128: kt * 512 + (gt + 1) * 128], xT[:, kt * 512:(kt + 1) * 512],
                       kt == 0, kt == 7, [twu, txT], [tb[bi]])
                dst = v3(uT[:, gt * 4096:(gt + 1) * 4096], 8)[:, :, j * 64:(j + 1) * 64]
                src = banks[bi][:].rearrange("p (c i) -> p i c", i=8)
                CP("dve" if gt % 2 == 0 else "act", dst, src, [tb[bi]], [tuT])
        dump("uT", uT, [128, 4 * 4096], BF16, tuT)
        S.barrier()
        A.release(m_persist)
        late_consts()
        if stop_after == "A_u":
            S.emit(fin)
            return nc, dbg_outs

        MUL, ADD, SUB = ALU.mult, ALU.add, ALU.subtract

        def sincos(theta, n, tP, scale_in=INV_2PI):
            y = A.alloc(n); yi = A.alloc(n).bitcast(I32); yf = A.alloc(n); r = A.alloc(n)
            sn = A.alloc(n); cs = A.alloc(n)
            TS("dve", y, theta, scale_in, None, MUL, None, [tP], [tP])
            CP("dve", yi, y, [tP], [tP]); CP("dve", yf, yi, [tP], [tP])
            TT("dve", r, y, yf, SUB, [tP], [tP])
            ACTF(sn, r, AF.Sin, [tP], [tP], scale=TWO_PI_LO)
            TS("dve", y, y, 0.25, None, ADD, None, [tP], [tP])
            CP("dve", yi, y, [tP], [tP]); CP("dve", yf, yi, [tP], [tP])
            TT("dve", r, y, yf, SUB, [tP], [tP])
            ACTF(cs, r, AF.Sin, [tP], [tP], scale=TWO_PI_LO)
            return sn, cs

        def ssm_params(srcs, npow):
            tP = T("ssmp")
            n = sum(k for _, k in srcs)
            lr = A.alloc(n); li = A.alloc(n); ls = A.alloc(n)
            off = 0
            for src, k in srcs:
                S.dma("sp", lr[:, off:off + k], src["lre"], writes=[tP]); S.dma("sp", li[:, off:off + k], src["lim"], writes=[tP])
                S.dma("sp", ls[:, off:off + k], src["lst"], writes=[tP])
                off += k
            dt_ = A.alloc(n); ACTF(dt_, ls, AF.Exp, [tP], [tP])
            ldr = A.alloc(n); TT("dve", ldr, lr, dt_, MUL, [tP], [tP])
            ldi = A.alloc(n); TT("dve", ldi, li, dt_, MUL, [tP], [tP])
            mag = A.alloc(n); ACTF(mag, ldr, AF.Exp, [tP], [tP])
            sn, cs = sincos(ldi, n, tP)
            ar = A.alloc(n); ai = A.alloc(n)
            TT("dve", ar, mag, cs, MUL, [tP], [tP]); TT("dve", ai, mag, sn, MUL, [tP], [tP])
            den = A.alloc(n); t0 = A.alloc(n); t1 = A.alloc(n)
            TT("dve", den, lr, lr, MUL, [tP], [tP]); TT("dve", t0, li, li, MUL, [tP], [tP])
            TT("dve", den, den, t0, ADD, [tP], [tP])
            S.op("dve", lambda e: e.reciprocal(out=den, in_=den), [tP], [tP])
            nr = A.alloc(n); TS("dve", nr, ar, -1.0, None, ADD, None, [tP], [tP])
            fr = A.alloc(n); fi = A.alloc(n)
            TT("dve", t0, nr, lr, MUL, [tP], [tP]); TT("dve", t1, ai, li, MUL, [tP], [tP])
            TT("dve", t0, t0, t1, ADD, [tP], [tP]); TT("dve", fr, t0, den, MUL, [tP], [tP])
            TT("dve", t0, ai, lr, MUL, [tP], [tP]); TT("dve", t1, nr, li, MUL, [tP], [tP])
            TT("dve", t0, t0, t1, SUB, [tP], [tP]); TT("dve", fi, t0, den, MUL, [tP], [tP])
            pw = []
            one = A.alloc(n); zer = A.alloc(n)
            MS("dve", one, 1.0, [tP]); MS("dve", zer, 0.0, [tP])
            pw.append((one, zer)); pw.append((ar, ai))
            for k in range(2, npow + 1):
                pr_, pi_ = pw[k - 1]
                nr_ = A.alloc(n); ni_ = A.alloc(n)
                TT("dve", t0, pr_, ar, MUL, [tP], [tP]); TT("dve", t1, pi_, ai, MUL, [tP], [tP])
                TT("dve", nr_, t0, t1, SUB, [tP], [tP])
                TT("dve", t0, pr_, ai, MUL, [tP], [tP]); TT("dve", t1, pi_, ar, MUL, [tP], [tP])
                TT("dve", ni_, t0, t1, ADD, [tP], [tP])
                pw.append((nr_, ni_))
            outs = []
            off = 0
            for _, k in srcs:
                sl_ = slice(off, off + k)
                outs.append(dict(tP=tP, fr=fr[:, sl_], fi=fi[:, sl_], mag=mag[:, sl_], ldi=ldi[:, sl_],
                                 pw=[(a_[:, sl_], b_[:, sl_]) for a_, b_ in pw]))
                off += k
            return outs

        Tz = A.alloc(4 * 8 * 128, BF16); tTz = T("Tz")
        Tz4 = Tz.rearrange("p (g t s) -> p g t s", g=4, t=8)
        Ww = A.alloc(16 * 8 * 2 * 32, BF16); tWw = T("Ww")
        Ww5 = Ww.rearrange("p (a j t s) -> p a j t s", a=16, j=8, t=2)
        MS("pool", Ww, 0.0, [tWw])
        r8 = A.alloc(16); psi = A.alloc(16); tsl = T("sl_small")
        cidx = A.alloc(512)
        cidx_i = A.alloc(512).bitcast(I32)
        S.op("pool", lambda e: e.iota(cidx_i, pattern=[[1, 512]], base=0, channel_multiplier=0), (), [tsl])
        CP("dve", cidx, cidx_i, [tsl], [tsl])
        Wz = A.alloc(4 * 8 * 2 * 128, BF16); tWz = T("Wz")
        Wz5 = Wz.rearrange("p (g i t s) -> p g i t s", g=4, i=8, t=2)
        m_tmp = A.mark()

        pr, pg, ps_ = ssm_params([(rl, 256), (gl, 32), (sl, 16)], 8)
        tRL = T("rl_tmp"); tGL = T("gl_tmp"); tSLt = T("sl_tmp")
        tP = pr["tP"]
        bre_t = A.alloc(256); bim_t = A.alloc(256)
        S.dma("sp", bre_t, rl["bre"], writes=[tRL]); S.dma("sp", bim_t, rl["bim"], writes=[tRL])
        bbr = A.alloc(256); bbi = A.alloc(256); t0 = A.alloc(256); t1 = A.alloc(256)
        TT("dve", t0, pr["fr"], bre_t, MUL, [tP, tRL], [tRL]); TT("dve", t1, pr["fi"], bim_t, MUL, [tP, tRL], [tRL])
        TT("dve", bbr, t0, t1, SUB, [tP, tRL], [tRL])
        TT("dve", t0, pr["fr"], bim_t, MUL, [tP, tRL], [tRL]); TT("dve", t1, pr["fi"], bre_t, MUL, [tP, tRL], [tRL])
        TT("dve", bbi, t0, t1, ADD, [tP, tRL], [tRL])
        vr = A.alloc(256); vi = A.alloc(256)
        for i in range(8):
            k = 7 - i
            if k == 0:
                svr, svi = bbr, bbi
            else:
                pwr, pwi = pr["pw"][k]
                TT("dve", t0, pwr, bbr, MUL, [tP, tRL], [tRL]); TT("dve", t1, pwi, bbi, MUL, [tP, tRL], [tRL])
                TT("dve", vr, t0, t1, SUB, [tP, tRL], [tRL])
                TT("dve", t0, pwr, bbi, MUL, [tP, tRL], [tRL]); TT("dve", t1, pwi, bbr, MUL, [tP, tRL], [tRL])
                TT("dve", vi, t0, t1, ADD, [tP, tRL], [tRL])
                svr, svi = vr, vi
            for part, sv in ((0, svr), (1, svi)):
                TS("dve", Wz5[:, :, i, part, 0:64], v3(sv, 4), even_c, None, MUL, None, [tP, tRL, tC], [tWz])
                TS("dve", Wz5[:, :, i, part, 64:128], v3(sv, 4), odd_c, None, MUL, None, [tP, tRL, tC], [tWz])
        dump("Wz", Wz, [128, 8192], BF16, tWz)

        xb = A.alloc(512); yb = A.alloc(512); cc = A.alloc(512); dsb = A.alloc(4)
        S.dma("sp", xb, gl_xb, writes=[tGL]); S.dma("sp", yb, gl_yb, writes=[tGL]); S.dma("sp", cc, gl_cc, writes=[tGL])
        S.dma("sp", dsb, d_rl, writes=[tGL])

        def bc_h(a):
            return a.unsqueeze(2).to_broadcast([128, 32, 16])
        xbb = A.alloc(512); ybb = A.alloc(512); g0 = A.alloc(512); g1 = A.alloc(512)
        TT("dve", g0, v3(xb, 32), bc_h(pg["fr"]), MUL, [tP, tGL], [tGL]); TT("dve", g1, v3(yb, 32), bc_h(pg["fi"]), MUL, [tP, tGL], [tGL])
        STT("dve", xbb, g1, sgn_c, g0, MUL, ADD, [tP, tGL, tC], [tGL])
        TT("dve", g0, v3(yb, 32), bc_h(pg["fr"]), MUL, [tP, tGL], [tGL]); TT("dve", g1, v3(xb, 32), bc_h(pg["fi"]), MUL, [tP, tGL], [tGL])
        STT("dve", ybb, g1, nsgn_c, g0, MUL, ADD, [tP, tGL, tC], [tGL])
        rhsC = A.alloc(512)
        TS("dve", rhsC, cc, nsgn_c, None, MUL, None, [tP, tGL, tC], [tGL])
        colg_i = A.alloc(128).bitcast(I32); colg = A.alloc(128); bdmask = A.alloc(128); identf = A.alloc(128)
        S.op("pool", lambda e: e.iota(colg_i, pattern=[[1, 8], [0, 16]], base=0, channel_multiplier=0), (), [tGL])
        CP("dve", colg, colg_i, [tP, tGL], [tGL])
        TS("dve", bdmask, colg, rowg_c, None, ALU.is_equal, None, [tP, tGL, tC], [tGL])
        CP("dve", identf, ident, [tC], [tGL])
        dm = A.alloc(512)
        for gt in range(4):
            TS("dve", dm[:, gt * 128:(gt + 1) * 128], identf, dsb[:, gt:gt + 1], None, MUL, None, [tP, tGL], [tGL])
        lhs = [A.alloc(512) for _ in range(2)]
        tz0 = A.alloc(512)
        for tau in range(8):
            lt = lhs[tau % 2]
            pwr, pwi = pg["pw"][tau]
            TT("dve", g0, v3(xbb, 32), bc_h(pwr), MUL, [tP, tGL], [tGL]); TT("dve", g1, v3(ybb, 32), bc_h(pwi), MUL, [tP, tGL], [tGL])
            tl = T("lhs%d" % tau)
            STT("dve", lt, g1, sgn_c, g0, MUL, ADD, [tP, tGL, tC], [tl])
            bi = nb()
            for gt in range(4):
                MM(banks[bi][:, gt * 128:(gt + 1) * 128], lt[:, gt * 128:(gt + 1) * 128], rhsC[:, gt * 128:(gt + 1) * 128],
                   True, True, [tl, tP, tGL], [tb[bi]])
            mb = bdmask.unsqueeze(1).to_broadcast([128, 4, 128])
            if tau == 0:
                TT("dve", v3(tz0, 4), v3(banks[bi][:], 4), mb, MUL, [tb[bi], tP, tGL], [tGL])
                TT("dve", Tz4[:, :, 0, :], v3(tz0, 4), v3(dm, 4), ADD, [tP, tGL], [tTz])
            else:
                TT("dve", Tz4[:, :, tau, :], v3(banks[bi][:], 4), mb, MUL, [tb[bi], tP, tGL], [tTz])
            S.op("dve", lambda e: e.memset(g0[:, 0:1], 0.0), [tl], [tGL])
        dump("Tz", Tz, [128, 4096], BF16, tTz)

        cre_t = A.alloc(256); cim_t = A.alloc(256)
        S.dma("sp", cre_t, sl_cre, writes=[tSLt]); S.dma("sp", cim_t, sl_cim, writes=[tSLt])
        m2 = A.alloc(16)
        TT("dve", m2, ps_["mag"], ps_["mag"], MUL, [tP, tSLt], [tSLt]); TT("dve", m2, m2, m2, MUL, [tP, tSLt], [tSLt])
        TT("dve", r8, m2, m2, MUL, [tP, tSLt], [tsl])
        y8 = A.alloc(16); y8i = A.alloc(16).bitcast(I32); y8f = A.alloc(16)
        TS("dve", y8, ps_["ldi"], 8.0 * INV_2PI, None, MUL, None, [tP, tSLt], [tSLt])
        CP("dve", y8i, y8, [tP, tSLt], [tSLt]); CP("dve", y8f, y8i, [tP, tSLt], [tSLt])
        TT("dve", psi, y8, y8f, SUB, [tP, tSLt], [tsl])
        tA_ = A.alloc(256); tB_ = A.alloc(256); tC_ = A.alloc(256); tD_ = A.alloc(256)

        def bc_o(a):
            return a.unsqueeze(2).to_broadcast([128, 16, 16])
        for j in range(8):
            pwr, pwi = ps_["pw"][j + 1]
            TT("pool", tA_, v3(cre_t, 16), bc_o(pwr), MUL, [tP, tSLt], [tSLt]); TT("pool", tB_, v3(cim_t, 16), bc_o(pwi), MUL, [tP, tSLt], [tSLt])
            TT("pool", tC_, v3(cre_t, 16), bc_o(pwi), MUL, [tP, tSLt], [tSLt]); TT("pool", tD_, v3(cim_t, 16), bc_o(pwr), MUL, [tP, tSLt], [tSLt])
            for half in range(2):
                hp = slice(half * 64, (half + 1) * 64)
                cs_ = slice(half * 16, (half + 1) * 16)
                TT("pool", Ww5[hp, :, j, 0, cs_], v3(tA_, 16)[hp], v3(tB_, 16)[hp], SUB, [tP, tSLt], [tWw])
                TT("pool", v3(tC_, 16)[hp], v3(tC_, 16)[hp], v3(tD_, 16)[hp], ADD, [tP, tSLt], [tSLt])
                TS("pool", Ww5[hp, :, j, 1, cs_], v3(tC_, 16)[hp], -1.0, None, MUL, None, [tP, tSLt], [tWw])
        dump("Ww", Ww, [128, 8192], BF16, tWw)
        S.dma("pool", s_glu, w_glu, writes=[tSglu], _bg=True)
        for r in range(2):
            S.dma("pool", s_qkv[r * 512:(r + 1) * 512, :], w_in[r * 512:(r + 1) * 512, 512:2048], writes=[tSqkv], _bg=True)
        S.barrier(); A.release(m_tmp)

        sprev = A.alloc(16 * 2 * 514, BF16); tsp = T("sprev")
        sprev4 = sprev.rearrange("p (a t c) -> p a t c", a=16, t=2)
        MS("dve", sprev, 0.0, [tsp])
        m_ssm = A.mark()
        uT4 = uT.rearrange("p (g i c) -> p g i c", g=4, i=8)
        Ers = [A.alloc(1024) for _ in range(2)]; Eis = [A.alloc(1024) for _ in range(2)]; tEs = [T("E0"), T("E1")]
        Y = A.alloc(1024); Yi = A.alloc(1024).bitcast(I32); Yf = A.alloc(1024); tY = T("Ytmp")
        wk = [[A.alloc(512) for _ in range(8)] for _ in range(2)]
        twk = [T("wk0"), T("wk1")]
        def egen(pb):
            Er = Ers[pb % 2]; Ei = Eis[pb % 2]; tE = tEs[pb % 2]
            TT("pool", v3(Y, 2), psi[:, pb * 2:(pb + 1) * 2].unsqueeze(2).to_broadcast([128, 2, 512]),
               cidx.unsqueeze(1).to_broadcast([128, 2, 512]), MUL, [tsl], [tY]); yield
            CP("dve", Yi, Y, [tY], [tY]); yield
            CP("dve", Yf, Yi, [tY], [tY]); yield
            TT("pool", Yf, Y, Yf, SUB, [tY], [tY]); yield
            ACTF(Ei, Yf, AF.Sin, [tY], [tE], scale=TWO_PI_LO); yield
            TS("dve", Y, Y, 0.25, None, ADD, None, [tY], [tY]); yield
            CP("dve", Yi, Y, [tY], [tY]); yield
            CP("dve", Yf, Yi, [tY], [tY]); yield
            TT("pool", Yf, Y, Yf, SUB, [tY], [tY]); yield
            ACTF(Er, Yf, AF.Sin, [tY], [tE], scale=TWO_PI_LO); yield

        for _ in egen(0):
            pass
        for pb in range(8):
            Er = Ers[pb % 2]; Ei = Eis[pb % 2]; tE = tEs[pb % 2]
            g_next = egen(pb + 1) if pb + 1 < 8 else iter(())

            def tick():
                next(g_next, None)
            for q in range(2):
                pt = pb * 2 + q
                rg = (pt % 4) * 32
                gt = pt // 4
                bzr, bzi = nb(), nb()
                for part, bz in ((0, bzr), (1, bzi)):
                    for i in range(8):
                        MM(banks[bz][:], Wz5[rg:rg + 32, gt, i, part, :], uT4[rg:rg + 32, gt, i, :], i == 0, i == 7,
                           [tWz, tuT], [tb[bz]], tile_position=(rg, 0))
                w = wk[pt % 2]; tw = twk[pt % 2]
                er = Er[:, q * 512:(q + 1) * 512]; ei = Ei[:, q * 512:(q + 1) * 512]
                TT("dve", w[0], banks[bzr][:], er, MUL, [tb[bzr], tE], [tw]); TT("dve", w[1], banks[bzi][:], ei, MUL, [tb[bzi], tE], [tw])
                tick()
                TT("dve", w[2], banks[bzi][:], er, MUL, [tb[bzi], tE], [tw]); TT("dve", w[3], banks[bzr][:], ei, MUL, [tb[bzr], tE], [tw])
                tick()
                TT("dve", w[4], w[0], w[1], ADD, [tw], [tw])
                TT("pool", w[5], w[2], w[3], SUB, [tw], [tw])
                tick()
                r8b = r8[:, pt:pt + 1].to_broadcast([128, 512])
                S.op("dve", lambda e, o=w[6], d0=r8b, d1=w[4]: e.tensor_tensor_scan(out=o, data0=d0, data1=d1, initial=0.0, op0=MUL, op1=ADD),
                     [tw, tsl], [tw])
                S.op("dve", lambda e, o=w[7], d0=r8b, d1=w[5]: e.tensor_tensor_scan(out=o, data0=d0, data1=d1, initial=0.0, op0=MUL, op1=ADD),
                     [tw, tsl], [tw])
                tick()
                TT("dve", w[0], w[6], er, MUL, [tw, tE], [tw]); TT("pool", w[1], w[7], ei, MUL, [tw, tE], [tw])
                TT("dve", w[2], w[7], er, MUL, [tw, tE], [tw]); TT("pool", w[3], w[6], ei, MUL, [tw, tE], [tw])
                tick()
                TT("dve", sprev4[:, pt, 0, 1:513], w[0], w[1], SUB, [tw], [tsp])
                TT("pool", sprev4[:, pt, 1, 1:513], w[2], w[3], ADD, [tw], [tsp])
            for _ in g_next:
                pass
        dump("sprev", sprev, [128, 16 * 2 * 514], BF16, tsp)
        S.barrier(); A.release(m_ssm)
        if stop_after == "S3":
            S.emit(fin)
            return nc, dbg_outs

        wglu = A.alloc(4 * 1024, BF16); twg = T("wglu")
        S.dma("sp", v3(wglu, 4), s_glu.rearrange("(kt p) c -> p kt c", p=128), reads=[tSglu], writes=[twg])
        yTg = [A.alloc(4 * 512, BF16) for _ in range(2)]; tyT = [T("yTg0"), T("yTg1")]
        NW = 3
        wk = [[A.alloc(512) for _ in range(5)] for _ in range(NW)]; twk = [T("w4_%d" % i) for i in range(NW)]
        sgb = [A.alloc(512) for _ in range(2)]; tsg = [T("sg0"), T("sg1")]
        AT4 = AT.rearrange("p (o c i) -> p o i c", o=4, i=8)
        cnt4 = [0]

        def s4(j):
            yt = yTg[j % 2]; tyt = tyT[j % 2]
            for gt in range(4):
                bI, bT = nb(), nb()
                for ptl in range(4):
                    pt = gt * 4 + ptl
                    for part in range(2):
                        MM(banks[bI][ptl * 32:(ptl + 1) * 32, :], Ww5[:, pt, j, part, :], sprev4[:, pt, part, 0:512],
                           part == 0, part == 1, [tWw, tsp], [tb[bI]], tile_position=(0, ptl * 32))
                for i in range(j + 1):
                    MM(banks[bT][:], Tz4[:, gt, j - i, :], uT4[:, gt, i, :], i == 0, i == j, [tTz, tuT], [tb[bT]])
                w = wk[cnt4[0] % NW]; tw = twk[cnt4[0] % NW]; cnt4[0] += 1
                CP("act", w[0], banks[bI][:], [tb[bI]], [tw])
                TT("dve", w[1], banks[bT][:], w[0], ADD, [tb[bT], tw], [tw])
                TT("pool", w[2], w[1], w[1], MUL, [tw], [tw])
                TS("pool", w[2], w[2], 0.044715, 1.0, MUL, ADD, [tw], [tw])
                TT("pool", w[3], w[2], w[1], MUL, [tw], [tw])
                ACTF(w[4], w[3], AF.Sigmoid, [tw], [tw], scale=2.0 * math.sqrt(2.0 / math.pi))
                TT("dve", yt[:, gt * 512:(gt + 1) * 512], w[1], w[4], MUL, [tw], [tyt])

        def s5(j):
            yt = yTg[j % 2]; tyt = tyT[j % 2]
            for ot in range(4):
                bA, bG = nb(), nb()
                for kt in range(4):
                    MM(banks[bA][:], wglu[:, kt * 1024 + ot * 128: kt * 1024 + (ot + 1) * 128], yt[:, kt * 512:(kt + 1) * 512],
                       kt == 0, kt == 3, [twg, tyt], [tb[bA]])
                for kt in range(4):
                    MM(banks[bG][:], wglu[:, kt * 1024 + 512 + ot * 128: kt * 1024 + 512 + (ot + 1) * 128],
                       yt[:, kt * 512:(kt + 1) * 512], kt == 0, kt == 3, [twg, tyt], [tb[bG]])
                sg = sgb[ot % 2]; ts_ = tsg[ot % 2]
                ACTF(sg, banks[bG][:], AF.Sigmoid, [tb[bG]], [ts_])
                TT("dve", AT4[:, ot, j, :], banks[bA][:], sg, MUL, [tb[bA], ts_], [tAT])

        s4(0)
        for j in range(8):
            if j + 1 < 8:
                s4(j + 1)
            s5(j)
        dump("AT", AT, [128, 4 * 4096], BF16, tAT)
        S.barrier(); A.release(m_keep)
        if stop_after == "B":
            S.emit(fin)
            return nc, dbg_outs

        qT = A.alloc(4 * 4096, BF16)
        m_bt = A.mark()
        kT = A.alloc(4 * 4096, BF16)
        v_sb = A.alloc(32 * 512, BF16)
        tq = [[T("q%d_%d" % (h, j)) for j in range(NT)] for h in range(4)]
        tk = [T("k%d" % j) for j in range(NT)]
        tv = [T("v%d" % j) for j in range(NT)]
        qT3 = v3(qT, 4); kT3 = v3(kT, 4)
        m_qkv = A.mark()
        cosT = A.alloc(1024); sinT = A.alloc(1024); tR = T("rope")
        m_r = A.mark()
        ti_i = A.alloc(32).bitcast(I32); tf_ = A.alloc(32); ji_i = A.alloc(32).bitcast(I32); jf_ = A.alloc(32); frq = A.alloc(32)
        S.op("pool", lambda e: e.iota(ti_i, pattern=[[128, 32]], base=0, channel_multiplier=1), (), [tR])
        S.op("pool", lambda e: e.iota(ji_i, pattern=[[1, 32]], base=0, channel_multiplier=0), (), [tR])
        CP("dve", tf_, ti_i, [tR], [tR]); CP("dve", jf_, ji_i, [tR], [tR])
        ACTF(frq, jf_, AF.Exp, [tR], [tR], scale=-math.log(10000.0) / 32.0)
        ang = A.alloc(1024)
        TT("dve", v3(ang, 32), tf_.unsqueeze(2).to_broadcast([128, 32, 32]), frq.unsqueeze(1).to_broadcast([128, 32, 32]),
           MUL, [tR], [tR])
        sn_, cs_ = sincos(ang, 1024, tR)
        CP("dve", sinT, sn_, [tR], [tR]); CP("dve", cosT, cs_, [tR], [tR])
        S.barrier(); A.release(m_r)
        cosT3 = v3(cosT, 32); sinT3 = v3(sinT, 32)

        cast_weights()
        wqkv = A.alloc(8 * 1536, BF16); twq = T("wqkv")
        for kt2 in range(4):
            S.dma("sp", v3(wqkv, 8)[:, 2 * kt2:2 * kt2 + 2, :],
                  s_qkv[kt2 * 256:(kt2 + 1) * 256, :].rearrange("(kt p) c -> p kt c", p=128), reads=[tSqkv], writes=[twq])
        xbf = A.alloc(4096, BF16); txbf = T("xbf")
        xTs = [A.alloc(4096, BF16) for _ in range(2)]; txTs = [T("xT0"), T("xT1")]
        qrs = [A.alloc(512, BF16) for _ in range(2)]; tqr = [T("qr0"), T("qr1")]
        rw = [[A.alloc(256) for _ in range(4)] for _ in range(2)]; trw = [T("rw0"), T("rw1")]
        def qk_proj(j, blk, which, cnt):
            xT = xTs[j % 2]; txT = txTs[j % 2]
            gb = j * 4 + blk
            cb = cosT3[:, gb, :].unsqueeze(1).to_broadcast([128, 8, 32])
            sb_ = sinT3[:, gb, :].unsqueeze(1).to_broadcast([128, 8, 32])
            bi = nb()
            for kt in range(8):
                MM(banks[bi][:], xT[:, kt * 512 + blk * 128: kt * 512 + (blk + 1) * 128],
                   wqkv[:, kt * 1536 + which * 512: kt * 1536 + (which + 1) * 512], kt == 0, kt == 7,
                   [txT, twq], [tb[bi]])
            b4 = banks[bi][:].rearrange("p (g t d) -> p g t d", g=8, t=2)
            t1 = b4[:, :, 0, :]; t2 = b4[:, :, 1, :]
            w = rw[cnt % 2]; tw = trw[cnt % 2]
            qr = qrs[cnt % 2]; tqr_ = tqr[cnt % 2]
            qr4 = qr.rearrange("p (g t d) -> p g t d", g=8, t=2)
            TT("dve", v3(w[0], 8), t1, cb, MUL, [tb[bi], tR], [tw]); TT("dve", v3(w[1], 8), t2, sb_, MUL, [tb[bi], tR], [tw])
            TT("dve", v3(w[2], 8), t2, cb, MUL, [tb[bi], tR], [tw]); TT("dve", v3(w[3], 8), t1, sb_, MUL, [tb[bi], tR], [tw])
            TT("dve", qr4[:, :, 0, :], v3(w[0], 8), v3(w[1], 8), SUB, [tw], [tqr_])
            TT("dve", qr4[:, :, 1, :], v3(w[2], 8), v3(w[3], 8), ADD, [tw], [tqr_])

        def qk_trans(j, blk, which, cnt):
            tok0 = j * 512 + blk * 128
            qr = qrs[cnt % 2]; tqr_ = tqr[cnt % 2]
            bi2 = nb()
            pb = banks[bi2][:].bitcast(BF16)
            for h in range(4):
                TR(pb[:, h * 128:(h + 1) * 128], qr[:, h * 128:(h + 1) * 128], ident, [tqr_, tC], [tb[bi2]])
            if which == 0:
                CP("act", qT3[:, :, tok0:tok0 + 128], v3(pb[:, 0:512], 4), [tb[bi2]], [tq[h][j] for h in range(4)])
            else:
                CP("act", kT3[:, :, tok0:tok0 + 128], v3(pb[:, 0:512], 4), [tb[bi2]], [tk[j]])

        def v_proj(j, blk):
            xT = xTs[j % 2]; txT = txTs[j % 2]
            gb = j * 4 + blk
            bi = nb()
            for kt in range(8):
                MM(banks[bi][:], xT[:, kt * 512 + blk * 128: kt * 512 + (blk + 1) * 128],
                   wqkv[:, kt * 1536 + 1024: kt * 1536 + 1536], kt == 0, kt == 7, [txT, twq], [tb[bi]])
            CP("act", v_sb[:, gb * 512:(gb + 1) * 512], banks[bi][:], [tb[bi]], [tv[j]])

        cnt = 0
        prev = None
        load_xT(0, xbf, txbf, xTs[0], txTs[0])
        for j in range(NT):
            if j + 1 < NT:
                load_xT(j + 1, xbf, txbf, xTs[(j + 1) % 2], txTs[(j + 1) % 2])
            for blk in range(4):
                for which in range(2):
                    qk_proj(j, blk, which, cnt)
                    if prev is not None:
                        qk_trans(*prev)
                    prev = (j, blk, which, cnt)
                    cnt += 1
                v_proj(j, blk)
        qk_trans(*prev)
        dump("qT", qT, [128, 4 * 4096], BF16, tq[0][0])
        dump("kT", kT, [128, 4 * 4096], BF16, tk[0])
        dump("v", v_sb, [128, 32 * 512], BF16, tv[0])
        S.barrier(); A.release(m_qkv)
        if stop_after == "A_qkv":
            S.emit(fin)
            return nc, dbg_outs

        Pb = [[A.alloc(512, BF16) for _ in range(2)] for _ in range(3)]
        tPb = [[T("P%d_%d" % (s_, c)) for c in range(2)] for s_ in range(3)]
        NE = 3
        ew = [[A.alloc(512) for _ in range(6)] for _ in range(NE)]; tew = [T("ew%d" % i) for i in range(NE)]
        esq = [A.alloc(512, BF16) for _ in range(NE)]
        its = [(h, jq, kb) for h in range(4) for jq in range(NT) for kb in range(4 * jq + 4)]

        def att_S(idx):
            h, jq, kb = its[idx]
            n0 = max(kb - 4 * jq, 0) * 128
            ks = slice(kb * 128, (kb + 1) * 128)
            for c in range(2):
                bS = 2 * (idx % 2) + c
                rs = slice(c * 64, (c + 1) * 64)
                MM(banks[bS][:, n0:512], kT3[rs, h, ks], qT3[rs, h, jq * 512 + n0:(jq + 1) * 512], True, True,
                   [tk[kb // 4], tq[h][jq]], [tb[bS]], tile_position=(c * 64, 0))

        def att_P(idx):
            h, jq, kb = its[idx]
            m = kb - 4 * jq
            n0 = max(m, 0) * 128
            for c in range(2):
                bS = 2 * (idx % 2) + c
                P = Pb[idx % 3][c]; tp = tPb[idx % 3][c]
                if n0 > 0:
                    MS("pool", P[:, 0:n0], 0.0, [tp])
                ACTF(P[:, n0:512], banks[bS][:, n0:512], AF.Exp, [tb[bS]], [tp], scale=0.125)
                if m >= 0:
                    TT("pool", P[:, n0:n0 + 128], P[:, n0:n0 + 128], tri, MUL, [tp, tC], [tp])

        def att_PV(idx):
            h, jq, kb = its[idx]
            nkb = 4 * jq + 4
            for c in range(2):
                P = Pb[idx % 3][c]; tp = tPb[idx % 3][c]
                MM(banks[4 + c][:], v_sb[:, kb * 512 + h * 128: kb * 512 + (h + 1) * 128], P, kb == 0, kb == nkb - 1,
                   [tv[kb // 4], tp], [tb[4 + c]])
                MM(banks[6 + c][:], ones_bf, P, kb == 0, kb == nkb - 1, [tC, tp], [tb[6 + c]])

        def att_epi_a(h, jq):
            e_ = (h * NT + jq) % NE
            w = ew[e_]; tw = tew[e_]
            CP("dve", w[2], banks[4][:], [tb[4]], [tw])
            CP("dve", w[0], banks[6][:], [tb[6]], [tw])
            CP("dve", w[3], banks[5][:], [tb[5]], [tw])
            CP("dve", w[1], banks[7][:], [tb[7]], [tw])

        def att_epi_b(h, jq, bank):
            e_ = (h * NT + jq) % NE
            w = ew[e_]; tw = tew[e_]; sq = esq[e_]
            qs = slice(jq * 512, (jq + 1) * 512)
            S.op("dve", lambda e, o=w[0]: e.reciprocal(out=o, in_=o), [tw], [tw])
            S.op("dve", lambda e, o=w[1]: e.reciprocal(out=o, in_=o), [tw], [tw])
            TT("dve", w[2], w[2], w[0], MUL, [tw], [tw])
            TT("dve", w[3], w[3], w[1], MUL, [tw], [tw])
            STT("dve", w[2], w[3], nlam_c, w[2], MUL, ADD, [tw, tC], [tw])
            TT("pool", sq, w[2], w[2], MUL, [tw], [tw])
            MM(banks[bank][:], onesm_bf, sq, True, True, [tC, tw], [tb[bank]])
            ACTF(w[4], banks[bank][:], AF.Ln, [tb[bank], tC], [tw], bias=eps_c)
            ACTF(w[5], w[4], AF.Exp, [tw], [tw], scale=-0.5)
            STT("dve", qT3[:, h, qs], w[2], subln_c, w[5], MUL, MUL, [tw, tC, tCd], [tq[h][jq]])

        pending = []
        att_S(0)
        for idx in range(len(its)):
            if idx + 1 < len(its):
                att_S(idx + 1)
            att_P(idx)
            att_PV(idx)
            h, jq, kb = its[idx]
            while pending and pending[0][0] <= idx:
                _, ph, pjq = pending.pop(0)
                att_epi_b(ph, pjq, 2 * (idx % 2))
            if kb == 4 * jq + 3:
                att_epi_a(h, jq)
                pending.append((idx + 8, h, jq))
        for (_, ph, pjq) in pending:
            att_epi_b(ph, pjq, 0)
        dump("BT", qT, [128, 4 * 4096], BF16, tq[3][7])
        S.barrier(); A.release(m_bt)
        if stop_after == "C":
            S.emit(fin)
            return nc, dbg_outs

        BT3 = qT3
        lnp = {}
        for nm, src in (("g1", ln1g), ("b1", ln1b), ("g2", ln2g), ("b2", ln2b)):
            lnp[nm] = A.alloc(DM)
            S.dma("sp", lnp[nm], src.to_broadcast([128, DM]), writes=[tC])
        hT = A.alloc(22 * 512, BF16); thT = T("hT")
        xT = A.alloc(4096, BF16); txT = T("xT")
        merged = A.alloc(4096, BF16); tmg = T("merged")
        xbf = merged
        x1 = A.alloc(4 * 1024); tx1 = [T("x1_%d" % b) for b in range(4)]
        x1bf = [A.alloc(1024, BF16) for _ in range(2)]; tx1bf = [T("x1bf0"), T("x1bf1")]
        gtmp = [[A.alloc(512) for _ in range(2)] for _ in range(2)]; tgt = [T("gt0"), T("gt1")]
        ftmp = [[A.alloc(512) for _ in range(2)] for _ in range(2)]; tft = [T("ft0"), T("ft1")]
        ysb = [A.alloc(1024) for _ in range(2)]; tys = [T("ys0"), T("ys1")]
        xblk = [A.alloc(1024) for _ in range(2)]; txb = [T("xb0"), T("xb1")]
        lnw = [A.alloc(20) for _ in range(2)]; tlnw = [T("lnw0"), T("lnw1")]
        NSLOT = 6
        slots = [A.alloc(2048, BF16) for _ in range(NSLOT)]; tslot = [T("slot%d" % i) for i in range(NSLOT)]
        slot_i = [0]

        def wload(src, a, tsrc):
            i = slot_i[0] % NSLOT
            slot_i[0] += 1
            view = v3(slots[i], a)
            S.dma("sp", view, src, reads=[tsrc], writes=[tslot[i]])
            return view, tslot[i]

        def ln_resid(src_res, tres, bank0, ys, ty):
            for half in range(2):
                hs = slice(half * 512, (half + 1) * 512)
                STT("dve", ys[:, hs], src_res[:, hs], ALPHA, banks[bank0 + half][:], MUL, ADD, [tres, tb[bank0 + half]], [ty])

        def layer_norm(blk, lni, src_res, tres, bank0, gname, bname, dst, tdst, ys=None, ty=None):
            lw = lnw[lni % 2]; tl = tlnw[lni % 2]
            if ys is None:
                ys = ysb[lni % 2]; ty = tys[lni % 2]
                ln_resid(src_res, tres, bank0, ys, ty)
            for half in range(2):
                S.op("dve", lambda e, o=lw[:, half * 6:(half + 1) * 6], i_=ys[:, half * 512:(half + 1) * 512]: e.bn_stats(out=o, in_=i_),
                     [ty], [tl])
            S.op("dve", lambda e, o=lw[:, 12:14], i_=v3(lw[:, 0:12], 2): e.bn_aggr(out=o, in_=i_), [tl], [tl])
            ACTF(lw[:, 14:15], lw[:, 13:14], AF.Sqrt, [tl, tC], [tl], bias=eps_c)
            S.op("dve", lambda e, o=lw[:, 15:16], i_=lw[:, 14:15]: e.reciprocal(out=o, in_=i_), [tl], [tl])
            STT("dve", ys, ys, lw[:, 12:13], lnp[gname], SUB, MUL, [ty, tl, tC], [ty])
            STT("dve", dst, ys, lw[:, 15:16], lnp[bname], MUL, ADD, [ty, tl, tC], [tdst])

        lni = 0
        gcnt = 0
        fcnt = 0
        pend_stores = []
        pend_ln2 = []
        lni2 = [1000]

        def ln2_rest(jj, blk):
            x1b = x1[:, blk * 1024:(blk + 1) * 1024]
            layer_norm(blk, lni2[0], None, None, None, "g2", "b2", x1b, tx1[blk], ys=x1b, ty=tx1[blk]); lni2[0] += 1
            pend_stores.append((out[jj * 512 + blk * 128: jj * 512 + (blk + 1) * 128, :], x1b, tx1[blk]))

        def flush_stores():
            while pend_stores:
                dst_, src_, t_ = pend_stores.pop(0)
                fin.append(S.dma("sp", dst_, src_, reads=[t_]))

        for j in range(NT):
            ts_ = slice(j * 512, (j + 1) * 512)
            if j == 0:
                load_xT(j, xbf, tmg, xT, txT)
            for dt_ in range(8):
                if dt_ % 2 == 0:
                    pu, tpu = wload(s_p[:, dt_ * 256: dt_ * 256 + 512].rearrange("(kt p) c -> p kt c", p=128), 4, tSp)
                gu, tgu = wload(s_g[:, dt_ * 256:(dt_ + 1) * 256].rearrange("(kt p) c -> p kt c", p=128), 8, tSg)
                bg_, bp_ = [nb(), nb()], [nb(), nb()]
                for br in range(2):
                    for kt in range(8):
                        MM(banks[bg_[br]][:], gu[:, kt, br * 128:(br + 1) * 128], xT[:, kt * 512:(kt + 1) * 512], kt == 0, kt == 7,
                           [tgu, txT], [tb[bg_[br]]])
                for br in range(2):
                    for kt in range(4):
                        if br == 0:
                            rhs = AT[:, kt * 4096 + j * 512: kt * 4096 + (j + 1) * 512]; tr_ = tAT
                        else:
                            rhs = BT3[:, kt, ts_]; tr_ = tq[kt][j]
                        c0 = (dt_ % 2) * 256 + br * 128
                        MM(banks[bp_[br]][:], pu[:, kt, c0:c0 + 128], rhs, kt == 0, kt == 3, [tpu, tr_], [tb[bp_[br]]])
                gw = gtmp[gcnt % 2]; tg_ = tgt[gcnt % 2]; gcnt += 1
                for br in range(2):
                    ACTF(gw[br], banks[bg_[br]][:], AF.Sigmoid, [tb[bg_[br]], tC, tCd], [tg_], bias=bg_sb[:, br * 8 + dt_: br * 8 + dt_ + 1])
                for br in range(2):
                    TT("dve", gw[br], banks[bp_[br]][:], gw[br], MUL, [tb[bp_[br]], tg_], [tg_])
                TT("pool", merged[:, dt_ * 512:(dt_ + 1) * 512], gw[0], gw[1], ADD, [tg_], [tmg])
                if dt_ % 2 == 1 and pend_ln2:
                    ln2_rest(*pend_ln2.pop(0))
            flush_stores()
            wos = [wload(s_o[u * 256:(u + 1) * 256, :].rearrange("(k p) c -> p k c", p=128), 2, tSo) for u in range(4)]
            def mix_mm(blk):
                for half in range(2):
                    bk = blk * 2 + half
                    for kt in range(8):
                        wo, two = wos[kt // 2]
                        MM(banks[bk][:], merged[:, kt * 512 + blk * 128: kt * 512 + (blk + 1) * 128],
                           wo[:, kt % 2, half * 512:(half + 1) * 512], kt == 0, kt == 7, [tmg, two], [tb[bk]])
            mix_mm(0)
            for blk in range(4):
                if blk + 1 < 4:
                    mix_mm(blk + 1)
                xb = xblk[blk % 2]; txb_ = txb[blk % 2]
                S.dma("sp", xb, x[j * 512 + blk * 128: j * 512 + (blk + 1) * 128, :], writes=[txb_])
                x1b = x1[:, blk * 1024:(blk + 1) * 1024]
                layer_norm(blk, lni, xb, txb_, blk * 2, "g1", "b1", x1b, tx1[blk]); lni += 1
                xh = x1bf[blk % 2]; txh = tx1bf[blk % 2]
                CP("act", xh, x1b, [tx1[blk]], [txh])
                bi = blk * 2
                pb = banks[bi][:].bitcast(BF16)
                for dt_ in range(8):
                    TR(pb[:, dt_ * 128:(dt_ + 1) * 128], xh[:, dt_ * 128:(dt_ + 1) * 128], ident, [txh, tC], [tb[bi]])
                CP("dve", v3(xT, 8)[:, :, blk * 128:(blk + 1) * 128], v3(pb, 8), [tb[bi]], [txT])
            for ot2 in range(11):
                gw_, tgw_ = wload(s_fg[:, ot2 * 256:(ot2 + 1) * 256].rearrange("(kt p) c -> p kt c", p=128), 8, tSfg)
                uw_, tuw_ = wload(s_fu[:, ot2 * 256:(ot2 + 1) * 256].rearrange("(kt p) c -> p kt c", p=128), 8, tSfu)
                for otl in range(2):
                    ot = ot2 * 2 + otl
                    ba, bb_ = nb(), nb()
                    for kt in range(8):
                        MM(banks[ba][:], gw_[:, kt, otl * 128:(otl + 1) * 128], xT[:, kt * 512:(kt + 1) * 512], kt == 0, kt == 7,
                           [tgw_, txT], [tb[ba]])
                    for kt in range(8):
                        MM(banks[bb_][:], uw_[:, kt, otl * 128:(otl + 1) * 128], xT[:, kt * 512:(kt + 1) * 512], kt == 0, kt == 7,
                           [tuw_, txT], [tb[bb_]])
                    fw_ = ftmp[fcnt % 2]; tf__ = tft[fcnt % 2]; fcnt += 1
                    ACTF(fw_[0], banks[ba][:], AF.Sigmoid, [tb[ba]], [tf__])
                    TT("dve", fw_[1], banks[ba][:], fw_[0], MUL, [tb[ba], tf__], [tf__])
                    TT("dve", hT[:, ot * 512:(ot + 1) * 512], banks[bb_][:], fw_[1], MUL, [tb[bb_], tf__], [thT])
            if j + 1 < NT:
                load_xT(j + 1, xbf, tmg, xT, txT)
            for u in range(11):
                dw, tdw = wload(s_fd[u * 256:(u + 1) * 256, :].rearrange("(k p) c -> p k c", p=128), 2, tSfd)
                for kl in range(2):
                    kt = 2 * u + kl
                    for blk in range(4):
                        for half in range(2):
                            bk = blk * 2 + half
                            MM(banks[bk][:], hT[:, kt * 512 + blk * 128: kt * 512 + (blk + 1) * 128],
                               dw[:, kl, half * 512:(half + 1) * 512], kt == 0, kt == 21, [thT, tdw], [tb[bk]])
            for blk in range(4):
                x1b = x1[:, blk * 1024:(blk + 1) * 1024]
                ln_resid(x1b, tx1[blk], blk * 2, x1b, tx1[blk])
            for blk in range(4):
                pend_ln2.append((j, blk))
            if j == NT - 1:
                while pend_ln2:
                    ln2_rest(*pend_ln2.pop(0))
        flush_stores()
        S.emit(fin)
    print("arena peak words", A.peak, "of", A.words, " ops", len(S.ops), flush=True)
    return nc, dbg_outs


def prep_common(inp):
    f = lambda a: np.ascontiguousarray(np.asarray(a, dtype=np.float32))
    c = {}
    c["w_in"] = f(inp["w_in"][0])
    c["bgate"] = f(inp["b_gate"][0].reshape(16, 128).T)
    lre, lim = inp["ssm_lambda_re"][0], inp["ssm_lambda_im"][0]
    lst = np.broadcast_to(inp["ssm_log_step"][0][:, None], (G, P_))
    bre, bim = inp["ssm_b_re"][0], inp["ssm_b_im"][0]
    cre, cim = inp["ssm_c_re"][0], inp["ssm_c_im"][0]
    dsk = inp["ssm_d"][0]

    def to_rl(a):
        a = np.asarray(a).reshape(4, 8, 1, P_)
        a = np.broadcast_to(a, (4, 8, H_, P_))
        return f(a.transpose(1, 2, 0, 3).reshape(128, 256))
    c["rl_lre"], c["rl_lim"], c["rl_lst"] = to_rl(lre), to_rl(lim), to_rl(lst)

    def b_rl(b):
        b = np.asarray(b).reshape(4, 8, P_, H_)
        return f(b.transpose(1, 3, 0, 2).reshape(128, 256))
    c["rl_bre"], c["rl_bim"] = b_rl(bre), b_rl(bim)
    c["d_rl"] = f(np.asarray(dsk).reshape(4, 8, H_).transpose(1, 2, 0).reshape(128, 4))

    def to_sl(a):
        a = np.asarray(a).reshape(16, 2, P_)
        return f(a.transpose(1, 2, 0).reshape(128, 16))
    c["sl_lre"], c["sl_lim"], c["sl_lst"] = to_sl(lre), to_sl(lim), to_sl(lst)

    def c_sl(cc):
        cc = np.asarray(cc).reshape(16, 2, H_, P_)
        return f(cc.transpose(1, 3, 0, 2).reshape(128, 256))
    c["sl_cre"], c["sl_cim"] = c_sl(cre), c_sl(cim)

    def to_gl(a):
        a = np.asarray(a).T
        return f(np.concatenate([a, a], axis=0))
    c["gl_lre"], c["gl_lim"], c["gl_lst"] = to_gl(lre), to_gl(lim), to_gl(lst)
    bre_t = np.asarray(bre).transpose(1, 0, 2).reshape(P_, G * H_)
    bim_t = np.asarray(bim).transpose(1, 0, 2).reshape(P_, G * H_)
    c["gl_xb"] = f(np.concatenate([bre_t, bim_t], axis=0))
    c["gl_yb"] = f(np.concatenate([bim_t, bre_t], axis=0))
    cre_t = np.asarray(cre).transpose(2, 0, 1).reshape(P_, G * H_)
    cim_t = np.asarray(cim).transpose(2, 0, 1).reshape(P_, G * H_)
    c["gl_cc"] = f(np.concatenate([cre_t, cim_t], axis=0))
    c["w_glu"] = f(inp["w_glu"][0])
    c["lamq1"] = f(inp["lambda_q1"][0][None, :]); c["lamk1"] = f(inp["lambda_k1"][0][None, :])
    c["lamq2"] = f(inp["lambda_q2"][0][None, :]); c["lamk2"] = f(inp["lambda_k2"][0][None, :])
    c["subln"] = f(inp["subln_gain"][0][:, None])
    c["w_ps"] = f(inp["w_proj_ssm"][0]); c["w_pa"] = f(inp["w_proj_attn"][0]); c["w_out"] = f(inp["w_out"][0])
    c["ln1g"] = f(inp["ln1_g"][0][None, :]); c["ln1b"] = f(inp["ln1_b"][0][None, :])
    c["ln2g"] = f(inp["ln2_g"][0][None, :]); c["ln2b"] = f(inp["ln2_b"][0][None, :])
    c["w_fg"] = f(inp["w_ffn_gate"][0]); c["w_fu"] = f(inp["w_ffn_up"][0]); c["w_fd"] = f(inp["w_ffn_down"][0])
    return c


_NC_CACHE = {}


def kernel(**inputs):
    if "nc" not in _NC_CACHE:
        _NC_CACHE["nc"] = build()[0]
    nc = _NC_CACHE["nc"]
    common = prep_common(inputs)
    xs = np.asarray(inputs["x"], dtype=np.float32)
    in_maps = []
    for b in range(8):
        m = dict(common)
        m["x"] = np.ascontiguousarray(xs[b])
        in_maps.append(m)
    res = run_bass_kernel_spmd(nc, in_maps, core_ids=list(range(8)))
    return np.stack([np.asarray(r["out"], dtype=np.float32) for r in res.results], axis=0)
```
